# Optimizing a Trainium2 kernel written in Bass

```python
import math
import jax, jax.numpy as jnp
from jax import lax
import numpy as np

D_MODEL = 1024
BATCH = 16
SEQ = 256
DEPTH = 2
DEC_BATCH = 4
DEC_SEQ = 1024
PAST_LEN = 256

GRID_W = 64
HEAD_DIM = 64
N_EVEN = (DEPTH + 1) // 2
N_ODD = DEPTH // 2
A_HEADS = 8
A_KV_HEADS = 2
A_GROUP = A_HEADS // A_KV_HEADS
B_HEADS = 8
B_DK = 64
B_DV = 64
C_HEADS = 4
C_VDIM = 2 * HEAD_DIM
D_HEADS = 8
D_KV_HEADS = 2
D_GROUP = D_HEADS // D_KV_HEADS
WINDOW = 128
Q_BLOCK = 128
RET_CHUNK = 128
ROPE_BASE = 10000.0
ROPE_PAIRS = HEAD_DIM // 4
EVEN_WIDTHS = (A_HEADS * HEAD_DIM, A_KV_HEADS * HEAD_DIM, A_KV_HEADS * HEAD_DIM,
               B_HEADS * B_DK, B_HEADS * B_DK, B_HEADS * B_DV, B_HEADS * B_DV)
EVEN_IN = A_HEADS * HEAD_DIM + 2 * A_KV_HEADS * HEAD_DIM + 2 * B_HEADS * B_DK + 2 * B_HEADS * B_DV
ODD_WIDTHS = (C_HEADS * 2 * HEAD_DIM, C_HEADS * 2 * HEAD_DIM, C_HEADS * C_VDIM,
              D_HEADS * HEAD_DIM, D_KV_HEADS * HEAD_DIM, D_KV_HEADS * HEAD_DIM)
ODD_IN = 4 * C_HEADS * HEAD_DIM + C_HEADS * C_VDIM + D_HEADS * HEAD_DIM + 2 * D_KV_HEADS * HEAD_DIM
MIX_WIDTH = A_HEADS * HEAD_DIM + B_HEADS * B_DV
PEER_HEADS = 8
PEER_NKEYS = 128
PEER_N = PEER_NKEYS * PEER_NKEYS
PEER_TOPK = 16
PEER_QDIM = 256
PEER_HALF = PEER_QDIM // 2
PEER_BLOCK = 128
EPS = 1e-6
NEG_INF = -1e30
F32 = jnp.float32

kernel_name = 'hybrid_diffusion_prefix_step'


def rmsnorm(x, g):
    xf = x.astype(F32)
    y = xf * lax.rsqrt(jnp.mean(xf * xf, axis=-1, keepdims=True) + EPS)
    return (y * g.astype(F32)).astype(x.dtype)


def head_norm(x):
    xf = x.astype(F32)
    mu = jnp.mean(xf, axis=-1, keepdims=True)
    var = jnp.mean(jnp.square(xf - mu), axis=-1, keepdims=True)
    return (xf - mu) * lax.rsqrt(var + EPS)


def ada(x, gain, shift, scale):
    return rmsnorm(x, gain) * (1.0 + scale[:, None, :]) + shift[:, None, :]


def split_cols(t, widths):
    points, acc = [], 0
    for w in widths[:-1]:
        acc += w
        points.append(acc)
    return jnp.split(t, points, axis=-1)


def to_heads(t, n_heads):
    bsz, L, _ = t.shape
    return t.reshape(bsz, L, n_heads, -1).transpose(0, 2, 1, 3)


def axial_rope_tables(n_tokens):
    rows = n_tokens // GRID_W
    row = jnp.repeat(jnp.arange(rows, dtype=F32), GRID_W)
    col = jnp.tile(jnp.arange(GRID_W, dtype=F32), rows)
    inv = ROPE_BASE ** (-jnp.arange(ROPE_PAIRS, dtype=F32) / ROPE_PAIRS)
    ang = jnp.concatenate([row[:, None] * inv, col[:, None] * inv], axis=-1)
    return jnp.cos(ang), jnp.sin(ang)


def apply_rope(x, cos, sin):
    half = x.shape[-1] // 2
    x1 = x[..., :half].astype(F32)
    x2 = x[..., half:].astype(F32)
    return jnp.concatenate([x1 * cos - x2 * sin, x1 * sin + x2 * cos], axis=-1).astype(x.dtype)


def block_attn(q, k, v, sink=None):
    bsz, hkv, grp, lq, d = q.shape
    nb = lq // Q_BLOCK
    scale = d ** -0.5
    qs = jnp.moveaxis(q.reshape(bsz, hkv, grp, nb, Q_BLOCK, d), 3, 0)

    def one_block(qb):
        s = jnp.einsum('bhgqd,bhkd->bhgqk', qb, k).astype(F32) * scale
        if sink is not None:
            s0 = jnp.broadcast_to(sink.astype(F32)[None, :, :, None, None], (bsz, hkv, grp, Q_BLOCK, 1))
            s = jnp.concatenate([s0, s], axis=-1)
        p = jax.nn.softmax(s, axis=-1)
        if sink is not None:
            p = p[..., 1:]
        return jnp.einsum('bhgqk,bhkd->bhgqd', p.astype(v.dtype), v)

    o = lax.map(one_block, qs)
    return jnp.moveaxis(o, 0, 3).reshape(bsz, hkv, grp, lq, d)


def diff_attn(q, k, v, lam):
    bsz, h, _, lq, d = q.shape
    nb = lq // Q_BLOCK
    qs = jnp.moveaxis(q.reshape(bsz, h, 2, nb, Q_BLOCK, d), 3, 0)

    def one_block(qb):
        s = jnp.einsum('bhcqd,bhckd->bhcqk', qb, k).astype(F32) * (d ** -0.5)
        p = jax.nn.softmax(s, axis=-1)
        a = p[:, :, 0] - lam * p[:, :, 1]
        return jnp.einsum('bhqk,bhkv->bhqv', a.astype(v.dtype), v)

    o = lax.map(one_block, qs)
    return jnp.moveaxis(o, 0, 2).reshape(bsz, h, lq, v.shape[-1])


def window_attn(q, k, v, k_ctx, v_ctx, sink):
    bsz, hkv, grp, L, d = q.shape
    nb = L // Q_BLOCK
    lc = k_ctx.shape[2]
    scale = d ** -0.5
    qb = q.reshape(bsz, hkv, grp, nb, Q_BLOCK, d)

    def band(t):
        tp = jnp.pad(t, ((0, 0), (0, 0), (Q_BLOCK, Q_BLOCK), (0, 0))).reshape(bsz, hkv, nb + 2, Q_BLOCK, d)
        return jnp.concatenate([tp[:, :, :-2], tp[:, :, 1:-1], tp[:, :, 2:]], axis=3)

    kb, vb = band(k), band(v)
    qi = jnp.arange(Q_BLOCK)
    kj = jnp.arange(3 * Q_BLOCK)
    rel = qi[:, None] + Q_BLOCK - kj[None, :]
    key_pos = jnp.arange(nb)[:, None] * Q_BLOCK - Q_BLOCK + kj[None, :]
    mask = (jnp.abs(rel) <= WINDOW)[None] & ((key_pos >= 0) & (key_pos < L))[:, None, :]
    s_band = jnp.where(mask, jnp.einsum('bhgnqd,bhnkd->bhgnqk', qb, kb).astype(F32) * scale, NEG_INF)
    s_ctx = jnp.einsum('bhgnqd,bhcd->bhgnqc', qb, k_ctx).astype(F32) * scale
    s_sink = jnp.broadcast_to(sink.astype(F32)[None, :, :, None, None, None], (bsz, hkv, grp, nb, Q_BLOCK, 1))
    p = jax.nn.softmax(jnp.concatenate([s_sink, s_ctx, s_band], axis=-1), axis=-1)
    o = (jnp.einsum('bhgnqc,bhcd->bhgnqd', p[..., 1:1 + lc].astype(v.dtype), v_ctx)
         + jnp.einsum('bhgnqk,bhnkd->bhgnqd', p[..., 1 + lc:].astype(v.dtype), vb))
    return o.reshape(bsz, hkv, grp, L, d)


def retention(q, k, v, log_gamma, s0, inclusive):
    bsz, h, L, _ = q.shape
    dv = v.shape[-1]
    n = L // RET_CHUNK
    idx = jnp.arange(RET_CHUNK, dtype=F32)
    diff = idx[:, None] - idx[None, :]
    keep = (diff >= 0) if inclusive else (diff > 0)
    intra = jnp.where(keep, jnp.exp(jnp.where(keep, diff, 0.0) * log_gamma[:, None, None]), 0.0)
    q_dec = jnp.exp((idx + 1.0) * log_gamma[:, None])
    k_dec = jnp.exp((RET_CHUNK - 1.0 - idx) * log_gamma[:, None])
    c_dec = jnp.exp(RET_CHUNK * log_gamma)

    def chunks(t):
        return jnp.moveaxis(t.astype(F32).reshape(bsz, h, n, RET_CHUNK, t.shape[-1]), 2, 0)

    def step(s, blk):
        qc, kc, vc = blk
        o = (jnp.einsum('bhqk,bhkv->bhqv', jnp.einsum('bhqd,bhkd->bhqk', qc, kc) * intra, vc)
             + jnp.einsum('bhqd,bhdv->bhqv', qc * q_dec[..., None], s))
        s = s * c_dec[:, None, None] + jnp.einsum('bhkd,bhkv->bhdv', kc * k_dec[..., None], vc)
        return s, o

    s, o = lax.scan(step, s0.astype(F32), (chunks(q), chunks(k), chunks(v)))
    return jnp.moveaxis(o, 0, 2).reshape(bsz, h, L, dv), s


def even_mixer(h, w_in, w_out, q_gain, k_gain, decay_fwd, decay_bwd, cache):
    bsz, L, _ = h.shape
    aq, ak, av, rq, rk, rv, rg = split_cols(h @ w_in, EVEN_WIDTHS)
    aq = rmsnorm(aq.reshape(bsz, L, A_KV_HEADS, A_GROUP, HEAD_DIM), q_gain).transpose(0, 2, 3, 1, 4)
    ak = rmsnorm(ak.reshape(bsz, L, A_KV_HEADS, HEAD_DIM), k_gain).transpose(0, 2, 1, 3)
    av = av.reshape(bsz, L, A_KV_HEADS, HEAD_DIM).transpose(0, 2, 1, 3)
    rq = to_heads(rq, B_HEADS)
    rk = to_heads(rk, B_HEADS) * (B_DK ** -0.5)
    rv = to_heads(rv, B_HEADS)
    lg_f = jax.nn.log_sigmoid(decay_fwd.astype(F32))
    lg_b = jax.nn.log_sigmoid(decay_bwd.astype(F32))
    if cache is None:
        a_out = block_attn(aq, ak, av)
        s0_f = jnp.zeros((bsz, B_HEADS, B_DK, B_DV), F32)
        s0_b = jnp.zeros((bsz, B_HEADS, B_DK, B_DV), F32)
    else:
        ck, cv, s0_f, s0_b = cache
        cos, sin = axial_rope_tables(L)
        a_out = block_attn(apply_rope(aq, cos, sin),
                           jnp.concatenate([ck, apply_rope(ak, cos, sin)], axis=2),
                           jnp.concatenate([cv, av], axis=2))
    o_f, s_f = retention(rq, rk, rv, lg_f, s0_f, True)
    o_b, s_b = retention(rq[:, :, ::-1], rk[:, :, ::-1], rv[:, :, ::-1], lg_b, s0_b, False)
    r = head_norm(o_f + o_b[:, :, ::-1]).astype(h.dtype)
    r = r.transpose(0, 2, 1, 3).reshape(bsz, L, B_HEADS * B_DV) * jax.nn.silu(rg)
    a_out = a_out.transpose(0, 3, 1, 2, 4).reshape(bsz, L, A_HEADS * HEAD_DIM)
    out = jnp.concatenate([a_out, r], axis=-1) @ w_out
    if cache is None:
        return out, (ak, av, s_f.astype(h.dtype), s_b.astype(h.dtype))
    return out, None


def odd_mixer(h, w_in, w_out, lq1, lk1, lq2, lk2, subln, sink, lam_init, cache):
    bsz, L, _ = h.shape
    cq, ck, cv, dq, dk, dv = split_cols(h @ w_in, ODD_WIDTHS)
    cq = cq.reshape(bsz, L, C_HEADS, 2, HEAD_DIM).transpose(0, 2, 3, 1, 4)
    ck = ck.reshape(bsz, L, C_HEADS, 2, HEAD_DIM).transpose(0, 2, 3, 1, 4)
    cv = cv.reshape(bsz, L, C_HEADS, C_VDIM).transpose(0, 2, 1, 3)
    dq = dq.reshape(bsz, L, D_KV_HEADS, D_GROUP, HEAD_DIM).transpose(0, 2, 3, 1, 4)
    dk = dk.reshape(bsz, L, D_KV_HEADS, HEAD_DIM).transpose(0, 2, 1, 3)
    dv = dv.reshape(bsz, L, D_KV_HEADS, HEAD_DIM).transpose(0, 2, 1, 3)
    lam = (jnp.exp(jnp.sum(lq1.astype(F32) * lk1.astype(F32)))
           - jnp.exp(jnp.sum(lq2.astype(F32) * lk2.astype(F32))) + lam_init)
    sink = sink.reshape(D_KV_HEADS, D_GROUP)
    if cache is None:
        c_out = diff_attn(cq, ck, cv, lam)
        d_out = block_attn(dq, dk, dv, sink)
        new_ctx = (ck.reshape(bsz, 2 * C_HEADS, L, HEAD_DIM), cv, dk, dv)
    else:
        cache_ck, cache_cv, cache_dk, cache_dv = cache
        lc = cache_ck.shape[2]
        cos, sin = axial_rope_tables(L)
        c_out = diff_attn(apply_rope(cq, cos, sin),
                          jnp.concatenate([cache_ck.reshape(bsz, C_HEADS, 2, lc, HEAD_DIM), apply_rope(ck, cos, sin)], axis=3),
                          jnp.concatenate([cache_cv, cv], axis=2), lam)
        d_out = window_attn(apply_rope(dq, cos, sin), apply_rope(dk, cos, sin), dv, cache_dk, cache_dv, sink)
        new_ctx = None
    c_out = rmsnorm(c_out, subln) * (1.0 - lam_init)
    c_out = c_out.transpose(0, 2, 1, 3).reshape(bsz, L, C_HEADS * C_VDIM)
    d_out = d_out.transpose(0, 3, 1, 2, 4).reshape(bsz, L, D_HEADS * HEAD_DIM)
    out = jnp.concatenate([c_out, d_out], axis=-1) @ w_out
    return out, new_ctx


def peer(h, w_q, sub_keys, u, v):
    bsz, L, dm = h.shape
    t = bsz * L
    x = h.reshape(t, dm)
    q = (x @ w_q).reshape(t, PEER_HEADS, 2, PEER_HALF)
    s = jnp.einsum('thcd,hcnd->thcn', q, sub_keys).astype(F32)
    s1, i1 = lax.top_k(s[:, :, 0], PEER_TOPK)
    s2, i2 = lax.top_k(s[:, :, 1], PEER_TOPK)
    cand = (s1[..., :, None] + s2[..., None, :]).reshape(t, PEER_HEADS, PEER_TOPK * PEER_TOPK)
    cidx = (i1[..., :, None] * PEER_NKEYS + i2[..., None, :]).reshape(t, PEER_HEADS, PEER_TOPK * PEER_TOPK)
    top_s, pos = lax.top_k(cand, PEER_TOPK)
    idx = jnp.take_along_axis(cidx, pos, axis=-1)
    g = jax.nn.softmax(top_s, axis=-1)
    nblk = t // PEER_BLOCK

    def one_block(blk):
        xb, ib, gb = blk
        a = jnp.einsum('td,thkd->thk', xb, u[ib])
        w = (gb * jax.nn.gelu(a.astype(F32))).astype(v.dtype)
        return jnp.einsum('thk,thkd->td', w, v[ib])

    out = lax.map(one_block, (x.reshape(nblk, PEER_BLOCK, dm),
                              idx.reshape(nblk, PEER_BLOCK, PEER_HEADS, PEER_TOPK),
                              g.reshape(nblk, PEER_BLOCK, PEER_HEADS, PEER_TOPK)))
    return out.reshape(bsz, L, dm)


def setup_inputs(seed: int = 0) -> dict:
    key = jax.random.key(seed)
    ks = iter(jax.random.split(key, 48))

    def nrm(shape, s):
        return s * jax.random.normal(next(ks), shape, F32)

    gam = 1.0 - 2.0 ** (-5.0 - np.arange(B_HEADS, dtype=np.float32))
    decay_logit = jnp.asarray(np.log(gam / (1.0 - gam)), dtype=F32)
    return {
        'x_prompt': nrm((BATCH, SEQ, D_MODEL), 1.0),
        'x_sample': nrm((DEC_BATCH, DEC_SEQ, D_MODEL), 1.0),
        'c': nrm((DEC_BATCH, D_MODEL), 1.0),
        'c_ctx': nrm((D_MODEL,), 1.0),
        'cache_a_k': nrm((DEC_BATCH, N_EVEN, A_KV_HEADS, PAST_LEN, HEAD_DIM), 1.0),
        'cache_a_v': nrm((DEC_BATCH, N_EVEN, A_KV_HEADS, PAST_LEN, HEAD_DIM), 1.0),
        'state_ret_fwd': nrm((DEC_BATCH, N_EVEN, B_HEADS, B_DK, B_DV), 0.5),
        'state_ret_bwd': nrm((DEC_BATCH, N_EVEN, B_HEADS, B_DK, B_DV), 0.5),
        'cache_c_k': nrm((DEC_BATCH, N_ODD, 2 * C_HEADS, PAST_LEN, HEAD_DIM), 1.0),
        'cache_c_v': nrm((DEC_BATCH, N_ODD, C_HEADS, PAST_LEN, C_VDIM), 1.0),
        'cache_d_k': nrm((DEC_BATCH, N_ODD, D_KV_HEADS, PAST_LEN, HEAD_DIM), 1.0),
        'cache_d_v': nrm((DEC_BATCH, N_ODD, D_KV_HEADS, PAST_LEN, HEAD_DIM), 1.0),
        'mod_w': nrm((DEPTH, D_MODEL, 6 * D_MODEL), 0.5 * D_MODEL ** -0.5),
        'mod_b': nrm((DEPTH, 6 * D_MODEL), 0.01),
        'norm_mix': 1.0 + nrm((DEPTH, D_MODEL), 0.05),
        'norm_ffn': 1.0 + nrm((DEPTH, D_MODEL), 0.05),
        'norm_final': 1.0 + nrm((D_MODEL,), 0.05),
        'ev_w_in': nrm((N_EVEN, D_MODEL, EVEN_IN), D_MODEL ** -0.5),
        'ev_w_out': nrm((N_EVEN, MIX_WIDTH, D_MODEL), MIX_WIDTH ** -0.5),
        'a_q_norm': 1.0 + nrm((N_EVEN, HEAD_DIM), 0.05),
        'a_k_norm': 1.0 + nrm((N_EVEN, HEAD_DIM), 0.05),
        'ret_decay_fwd': decay_logit + nrm((N_EVEN, B_HEADS), 0.05),
        'ret_decay_bwd': decay_logit + nrm((N_EVEN, B_HEADS), 0.05),
        'od_w_in': nrm((N_ODD, D_MODEL, ODD_IN), D_MODEL ** -0.5),
        'od_w_out': nrm((N_ODD, MIX_WIDTH, D_MODEL), MIX_WIDTH ** -0.5),
        'c_lambda_q1': nrm((N_ODD, HEAD_DIM), 0.1),
        'c_lambda_k1': nrm((N_ODD, HEAD_DIM), 0.1),
        'c_lambda_q2': nrm((N_ODD, HEAD_DIM), 0.1),
        'c_lambda_k2': nrm((N_ODD, HEAD_DIM), 0.1),
        'c_subln': 1.0 + nrm((N_ODD, C_VDIM), 0.05),
        'd_sink': nrm((N_ODD, D_HEADS), 0.5),
        'peer_w_q': nrm((DEPTH, D_MODEL, PEER_HEADS * PEER_QDIM), D_MODEL ** -0.5),
        'peer_subkeys': nrm((DEPTH, PEER_HEADS, 2, PEER_NKEYS, PEER_HALF), PEER_HALF ** -0.5),
        'peer_u': nrm((DEPTH, PEER_N, D_MODEL), D_MODEL ** -0.5),
        'peer_v': nrm((DEPTH, PEER_N, D_MODEL), 0.25),
    }


def reference(x_prompt, x_sample, c, c_ctx,
              cache_a_k, cache_a_v, state_ret_fwd, state_ret_bwd,
              cache_c_k, cache_c_v, cache_d_k, cache_d_v,
              mod_w, mod_b, norm_mix, norm_ffn, norm_final,
              ev_w_in, ev_w_out, a_q_norm, a_k_norm, ret_decay_fwd, ret_decay_bwd,
              od_w_in, od_w_out, c_lambda_q1, c_lambda_k1, c_lambda_q2, c_lambda_k2, c_subln, d_sink,
              peer_w_q, peer_subkeys, peer_u, peer_v):
    xp, xs = x_prompt, x_sample
    new_ak, new_av, new_sf, new_sb = [], [], [], []
    new_ck, new_cv, new_dk, new_dv = [], [], [], []
    for layer in range(DEPTH):
        e = layer // 2
        mod_p = jnp.split(jax.nn.silu(c_ctx)[None, :] @ mod_w[layer] + mod_b[layer], 6, axis=-1)
        mod_s = jnp.split(jax.nn.silu(c) @ mod_w[layer] + mod_b[layer], 6, axis=-1)
        hp = ada(xp, norm_mix[layer], mod_p[0], mod_p[1])
        hs = ada(xs, norm_mix[layer], mod_s[0], mod_s[1])
        if layer % 2 == 0:
            out_p, ctx = even_mixer(hp, ev_w_in[e], ev_w_out[e], a_q_norm[e], a_k_norm[e],
                                    ret_decay_fwd[e], ret_decay_bwd[e], None)
            new_ak.append(ctx[0]); new_av.append(ctx[1]); new_sf.append(ctx[2]); new_sb.append(ctx[3])
            out_s, _ = even_mixer(hs, ev_w_in[e], ev_w_out[e], a_q_norm[e], a_k_norm[e],
                                  ret_decay_fwd[e], ret_decay_bwd[e],
                                  (cache_a_k[:, e], cache_a_v[:, e], state_ret_fwd[:, e], state_ret_bwd[:, e]))
        else:
            lam_init = 0.8 - 0.6 * math.exp(-0.3 * layer)
            out_p, ctx = odd_mixer(hp, od_w_in[e], od_w_out[e], c_lambda_q1[e], c_lambda_k1[e],
                                   c_lambda_q2[e], c_lambda_k2[e], c_subln[e], d_sink[e], lam_init, None)
            new_ck.append(ctx[0]); new_cv.append(ctx[1]); new_dk.append(ctx[2]); new_dv.append(ctx[3])
            out_s, _ = odd_mixer(hs, od_w_in[e], od_w_out[e], c_lambda_q1[e], c_lambda_k1[e],
                                 c_lambda_q2[e], c_lambda_k2[e], c_subln[e], d_sink[e], lam_init,
                                 (cache_c_k[:, e], cache_c_v[:, e], cache_d_k[:, e], cache_d_v[:, e]))
        xp = xp + mod_p[2][:, None, :] * out_p
        xs = xs + mod_s[2][:, None, :] * out_s
        xp = xp + mod_p[5][:, None, :] * peer(ada(xp, norm_ffn[layer], mod_p[3], mod_p[4]),
                                              peer_w_q[layer], peer_subkeys[layer], peer_u[layer], peer_v[layer])
        xs = xs + mod_s[5][:, None, :] * peer(ada(xs, norm_ffn[layer], mod_s[3], mod_s[4]),
                                              peer_w_q[layer], peer_subkeys[layer], peer_u[layer], peer_v[layer])
    y_prompt = rmsnorm(xp, norm_final)
    y_sample = rmsnorm(xs, norm_final)
    new_a_k = jnp.stack(new_ak, axis=1)
    new_a_v = jnp.stack(new_av, axis=1)
    new_ret_fwd = jnp.stack(new_sf, axis=1)
    new_ret_bwd = jnp.stack(new_sb, axis=1)
    new_c_k = jnp.stack(new_ck, axis=1)
    new_c_v = jnp.stack(new_cv, axis=1)
    new_d_k = jnp.stack(new_dk, axis=1)
    new_d_v = jnp.stack(new_dv, axis=1)
    return (y_prompt, y_sample, new_a_k, new_a_v, new_ret_fwd, new_ret_bwd, new_c_k, new_c_v, new_d_k, new_d_v)
```

```python
import os
import math
import contextlib
import numpy as np
import ml_dtypes
import concourse.bass as bass
import concourse.mybir as mybir
from concourse.bass_utils import run_bass_kernel_spmd

F32 = mybir.dt.float32
BF16 = mybir.dt.bfloat16
I32 = mybir.dt.int32
AF = mybir.ActivationFunctionType
ALU = mybir.AluOpType
AX = mybir.AxisListType

PE, ACT, DVE, POOL, SP = "tensor", "scalar", "vector", "gpsimd", "sync"
ENGINES = (PE, ACT, DVE, POOL, SP)
SEM_CHUNK = 20000
DMA_RING = 12
EPS = 1e-6
NEG = -30000.0
DEBUG = bool(int(os.environ.get("MK_DEBUG", "0")))
STOP_AFTER = os.environ.get("MK_STOP", "")
STRICT = bool(int(os.environ.get("MK_STRICT", "1")))
SMALL_UV = STOP_AFTER in ("mix0", "mix1", "l1only")


class Op:
    __slots__ = ("eng", "fn", "is_dma", "deps", "signal", "idx", "sig_no", "dma_no")

    def __init__(self, eng, fn, is_dma):
        self.eng = eng
        self.fn = fn
        self.is_dma = is_dma
        self.deps = []
        self.signal = False
        self.sig_no = -1
        self.dma_no = -1


class Prog:
    def __init__(self, nc):
        self.nc = nc
        self.ops = []
        self.last_writer = {}
        self.readers = {}
        self.out_dmas = []
        self.group_last = {}
        self.group_dmas = {}
        self.fence = {}

    def retire(self, *groups):
        for g in groups:
            self.fence[g] = list(self.group_last.get(g, {}).values()) + self.group_dmas.get(g, [])
            self.group_last[g] = {}
            self.group_dmas[g] = []

    def op(self, eng, fn, reads=(), writes=(), dma=False):
        o = Op(eng, fn, dma)
        o.idx = len(self.ops)
        deps = {}
        groups = set()
        for k in list(reads) + list(writes):
            i = k.find(":")
            if i >= 0:
                groups.add(k[:i])
        for g in groups:
            for p in self.fence.get(g, ()):
                if p.idx not in deps:
                    deps[p.idx] = (p, False)
        for r in reads:
            w = self.last_writer.get(r)
            if w is not None:
                deps[w.idx] = (w, True)
        for k in writes:
            w = self.last_writer.get(k)
            if w is not None and w.idx not in deps:
                deps[w.idx] = (w, False)
            for rd in self.readers.get(k, ()):
                if rd.idx not in deps:
                    deps[rd.idx] = (rd, False)
        for (p, raw) in deps.values():
            if p.eng == o.eng and not p.is_dma and not o.is_dma:
                if o.eng == PE or (not raw and not STRICT):
                    continue
            o.deps.append(p)
        for r in reads:
            self.readers.setdefault(r, []).append(o)
        for k in writes:
            self.last_writer[k] = o
            self.readers[k] = []
        for g in groups:
            if dma:
                self.group_dmas.setdefault(g, []).append(o)
            else:
                self.group_last.setdefault(g, {})[eng] = o
        self.ops.append(o)
        return o

    def dma(self, eng, out, in_, reads=(), writes=(), is_output=False, **kw):
        o = self.op(eng, lambda e: e.dma_start(out=out, in_=in_, **kw), reads, writes, dma=True)
        if is_output:
            self.out_dmas.append(o)
        return o

    def emit(self):
        nc = self.nc
        for o in self.ops:
            for p in o.deps:
                p.signal = True
            if o.is_dma:
                o.signal = True
        n_sig = {e: 0 for e in ENGINES}
        n_dma = {e: 0 for e in ENGINES}
        for o in self.ops:
            if o.is_dma:
                o.dma_no = n_dma[o.eng]
                n_dma[o.eng] += 1
            elif o.signal:
                o.sig_no = n_sig[o.eng]
                n_sig[o.eng] += 1
        es = contextlib.ExitStack()
        sems, dsems = {}, {}
        for e in ENGINES:
            n = (n_sig[e] + SEM_CHUNK - 1) // SEM_CHUNK
            sems[e] = [es.enter_context(nc.semaphore(f"s_{e}_{i}")) for i in range(n)]
            if n_dma[e]:
                dsems[e] = [es.enter_context(nc.semaphore(f"d_{e}_{i}")) for i in range(min(DMA_RING, n_dma[e]))]

        def target(p):
            if p.is_dma:
                return dsems[p.eng][p.dma_no % DMA_RING], 16 * (p.dma_no // DMA_RING + 1)
            return sems[p.eng][p.sig_no // SEM_CHUNK], p.sig_no % SEM_CHUNK + 1

        per_eng = {e: [o for o in self.ops if o.eng == e] for e in ENGINES}
        final_waits = [target(o) for o in self.out_dmas]

        def run(eng_name, eng):
            seen = {}
            for o in per_eng[eng_name]:
                waits = {}
                for p in o.deps:
                    s, v = target(p)
                    if seen.get(s.name, 0) >= v:
                        continue
                    if waits.get(s.name, (None, 0))[1] < v:
                        waits[s.name] = (s, v)
                if o.is_dma and o.dma_no >= DMA_RING:
                    s = dsems[eng_name][o.dma_no % DMA_RING]
                    v = 16 * (o.dma_no // DMA_RING)
                    if seen.get(s.name, 0) < v and waits.get(s.name, (None, 0))[1] < v:
                        waits[s.name] = (s, v)
                for (s, v) in waits.values():
                    eng.wait_ge(s, v)
                    seen[s.name] = v
                inst = o.fn(eng)
                if o.signal:
                    s, v = target(o)
                    inst.then_inc(s, 16 if o.is_dma else 1)
            if eng_name == SP:
                done = {}
                for (s, v) in final_waits:
                    if done.get(s.name, 0) < v:
                        done[s.name] = v
                for (s, v) in final_waits:
                    if done.get(s.name) == v:
                        eng.wait_ge(s, v)
                        done[s.name] = -1

        with nc.Block() as block:
            @block.sync
            def _(e):
                run(SP, e)

            @block.scalar
            def _(e):
                run(ACT, e)

            @block.vector
            def _(e):
                run(DVE, e)

            @block.gpsimd
            def _(e):
                run(POOL, e)

            @block.tensor
            def _(e):
                run(PE, e)
        es.close()
        return {e: len(per_eng[e]) for e in ENGINES}


INPUT_SPECS = [
    ("x", [1024, 1024], F32), ("cvec", [1, 1024], F32),
    ("mod_w", [2, 1024, 6144], F32), ("mod_b", [2, 6144], F32),
    ("norm_mix", [2, 1024], F32), ("norm_ffn", [2, 1024], F32), ("norm_final", [1, 1024], F32),
    ("ev_w_in", [1024, 2816], F32), ("ev_w_out", [1024, 1024], F32),
    ("a_q_norm", [1, 64], F32), ("a_k_norm", [1, 64], F32), ("decay", [1, 16], F32),
    ("od_w_in", [1024, 2304], F32), ("od_w_out", [1024, 1024], F32),
    ("lam4", [4, 64], F32), ("subln", [1, 128], F32), ("sink", [1, 8], F32),
    ("w_q", [2, 1024, 2048], F32), ("subkeys", [2, 16, 128, 128], F32),
    ("u0", [16384, 1024], F32), ("u1", [16384, 1024], F32), ("v0", [16384, 1024], F32), ("v1", [16384, 1024], F32),
    ("cak", [2, 256, 64], F32), ("cav", [2, 256, 64], F32), ("s0f", [8, 64, 64], F32), ("s0b", [8, 64, 64], F32),
    ("cck", [8, 256, 64], F32), ("ccv", [4, 256, 128], F32), ("cdk", [2, 256, 64], F32), ("cdv", [2, 256, 64], F32),
    ("rope_cos", [1024, 32], F32), ("rope_sin", [1024, 32], F32),
    ("mvA", [10, 1024], BF16), ("mvD", [10, 1024], BF16), ("triA", [128, 128], BF16), ("triB", [128, 128], BF16),
    ("sel", [10, 10, 128], BF16), ("selK", [10, 1280], BF16), ("identb", [128, 128], BF16), ("identf", [128, 128], F32),
    ("flags", [1, 16], F32), ("diff", [128, 128], F32), ("keepf", [128, 128], F32), ("keepb", [128, 128], F32),
    ("cols", [128, 4], F32), ("iota128", [128, 128], I32),
]
OUTPUT_SPECS = [
    ("y", [1024, 1024]), ("nak", [2, 1024, 64]), ("nav", [2, 1024, 64]),
    ("nsf", [4, 8, 64, 64]), ("nsb", [4, 8, 64, 64]),
    ("nck", [8, 1024, 64]), ("ncv", [4, 1024, 128]), ("ndk", [2, 1024, 64]), ("ndv", [2, 1024, 64]),
]
DEBUG_SPECS = [("dbg0", [1024, 1024]), ("dbg1", [1024, 1024]), ("dbg2", [1024, 1024]), ("dbg3", [1024, 1024]), ("dbgm", [1024, 1024])]


def build_program():
    nc = bass.Bass("TRN2", target_bir_lowering=False)
    D = {}
    for (n, shp, dt) in INPUT_SPECS:
        if SMALL_UV and n in ("u0", "u1", "v0", "v1"):
            shp = [128, 1024]
        D[n] = nc.dram_tensor(n, shp, dt, kind="ExternalInput").ap()
    for (n, shp) in OUTPUT_SPECS + (DEBUG_SPECS if DEBUG else []):
        D[n] = nc.dram_tensor(n, shp, F32, kind="ExternalOutput").ap()
    es = contextlib.ExitStack()
    P = Prog(nc)

    def sb(name, shape, dt=F32):
        return es.enter_context(nc.sbuf_tensor("s_" + name, shape, dt))

    def psum(name, shape, dt=F32):
        return es.enter_context(nc.psum_tensor("p_" + name, shape, dt))

    def V(fn, r=(), w=()):
        return P.op(DVE, fn, r, w)

    def A(fn, r=(), w=()):
        return P.op(ACT, fn, r, w)

    def T(fn, r=(), w=()):
        return P.op(PE, fn, r, w)

    def G(fn, r=(), w=()):
        return P.op(POOL, fn, r, w)

    xres = sb("xres", [128, 8, 1024])
    modbc = sb("modbc", [128, 6144])
    hT = sb("hT", [128, 8, 1024], BF16)
    identb = sb("identb", [128, 128], BF16)
    identf = sb("identf", [128, 128])
    cos_t = sb("cos_t", [128, 8, 32])
    sin_t = sb("sin_t", [128, 8, 32])
    sel = sb("sel", [10, 10, 128], BF16)
    mvA = sb("mvA", [10, 1024], BF16)
    mvD = sb("mvD", [10, 1024], BF16)
    triA = sb("triA", [128, 128], BF16)
    triB = sb("triB", [128, 128], BF16)
    cols = sb("cols", [128, 4])
    iota = sb("iota", [128, 128], I32)
    small = sb("small", [128, 512])
    R_Q = sb("R_Q", [128, 8192], BF16)
    R_K = sb("R_K", [128, 10240], BF16)
    R_M = sb("R_M", [128, 4096], BF16)
    R_X = sb("R_X", [128, 8192], BF16)
    R_V = sb("R_V", [128, 5376], BF16)
    R_G = sb("R_G", [128, 4096], BF16)
    R_W = sb("R_W", [128, 8192], BF16)
    R_T = sb("R_T", [128, 4096])
    R_P = sb("R_P", [128, 2048], BF16)
    R_S = sb("R_S", [128, 1024])
    if DEBUG:
        print("sbuf bytes remaining/partition:", nc.sbuf_bytes_remaining)

    psA = psum("psA", [128, 2048])
    psB0 = psum("psB0", [128, 512])
    psB1 = psum("psB1", [128, 512])
    psT0 = psum("psT0", [128, 1024], BF16)
    psT1 = psum("psT1", [128, 1024], BF16)
    psS = [psA[:, 0:512], psA[:, 512:1024]]
    psO = psA[:, 1024:2048]
    psB = [psB0[:, :], psB1[:, :]]
    accO = [psA[:, 1024:1536], psA[:, 1536:2048], psB[0], psB[1]]
    ACCK = ["psO", "psB0", "psB1"]
    pO4 = psA[:, 1536:2048]

    tmpf = [R_T[:, i * 1024:(i + 1) * 1024] for i in range(4)]
    wblk = [R_W[:, i * 4096:(i + 1) * 4096].rearrange("p (j n) -> p j n", j=8) for i in range(2)]
    wfull = R_W[:, :].rearrange("p (j n) -> p j n", j=8)
    mixv = R_X[:, :].rearrange("p (t f) -> p t f", t=8)
    QT = R_Q[0:64, :].rearrange("p (h t) -> p h t", h=8)
    QTa = R_Q[0:74, :].rearrange("p (h t) -> p h t", h=8)
    KTa = R_K[0:74, :].rearrange("p (h t) -> p h t", h=8)
    KT = R_K[0:64, :].rearrange("p (h t) -> p h t", h=8)
    PTb = [R_P[:, i * 512:(i + 1) * 512] for i in range(2)]
    hbuf = [R_P[:, 1024:2048], R_G[:, 0:1024]]
    hbufk = ["hbuf0", "G:hbuf1"]

    _sm = [0]

    def sm(n):
        a = small[:, _sm[0]:_sm[0] + n]
        _sm[0] += n
        assert _sm[0] <= 512
        return a

    ssq = sm(8); std = sm(8); rstd = sm(8)
    hss = sm(8); hstd = sm(8); hrs = sm(8)
    dec = sm(16); lgt = sm(16); nlg = sm(16)
    qdf = sm(8); qdb = sm(8); kdf = sm(8); kdb = sm(8); cdf = sm(8); cdb = sm(8)
    flg = sm(16); esink = sm(8); sinkb = sm(8)
    lamt = sm(8); dent = sm(8); rdent = sm(8)
    musum = sm(8); negmu = sm(8); varsum = sm(8); vstd = sm(8); vrs = sm(8)
    gq = sb("gq", [128, 512]); gk = sb("gk", [128, 128])
    csil = sb("csil", [128, 8])
    csilbc = R_X[:, 0:1024].rearrange("p (j n) -> p j n", j=8)

    uid = [0]

    def U(prefix):
        uid[0] += 1
        return f"{prefix}#{uid[0]}"

    P.dma(SP, identb[:], D["identb"], writes=["identb"])
    P.dma(SP, identf[:], D["identf"], writes=["identf"])
    for t in range(8):
        P.dma(SP, xres[:, t, :], D["x"][t * 128:(t + 1) * 128, :], writes=[f"x{t}"])
    P.dma(SP, cos_t[:], D["rope_cos"].rearrange("(t p) d -> p t d", p=128), writes=["cos"])
    P.dma(SP, sin_t[:], D["rope_sin"].rearrange("(t p) d -> p t d", p=128), writes=["sin"])
    P.dma(SP, sel[:], D["sel"], writes=["sel"])
    P.dma(SP, mvA[:], D["mvA"], writes=["mvA"])
    P.dma(SP, mvD[:], D["mvD"], writes=["mvD"])
    P.dma(SP, triA[:], D["triA"], writes=["triA"])
    P.dma(SP, triB[:], D["triB"], writes=["triB"])
    P.dma(SP, cols[:], D["cols"], writes=["cols"])
    P.dma(SP, iota[:], D["iota128"], writes=["iota"])
    P.dma(SP, flg, D["flags"].partition_broadcast(128), writes=["flg"])
    P.dma(SP, csil[:], D["cvec"].rearrange("o (j d) -> d (o j)", d=128), writes=["csil"], allow_slow_non_contiguous=True)
    A(lambda e: e.activation(out=csil[:], in_=csil[:], func=AF.Silu), ["csil"], ["csil"])

    def dbg_dump(k):
        if not DEBUG:
            return
        for t in range(8):
            P.dma(SP, D[f"dbg{k}"][t * 128:(t + 1) * 128, :], xres[:, t, :], reads=[f"x{t}"], is_output=True)

    def compute_mod(l):
        modw_v = D["mod_w"][l].rearrange("(j d) n -> d j n", d=128)
        mkeys = [f"modbc{b}" for b in range(12)]
        V(lambda e: e.tensor_copy(out=csilbc, in_=csil[:].unsqueeze(2).to_broadcast([128, 8, 128])), ["csil"], ["X:csilbc"])
        P.dma(SP, modbc[:, :], D["mod_b"][l:l + 1, :].partition_broadcast(128), writes=mkeys)
        stage = [R_T[:, 0:2048].bitcast(BF16).rearrange("p (j n) -> p j n", j=8),
                 R_T[:, 2048:4096].bitcast(BF16).rearrange("p (j n) -> p j n", j=8)]
        skeys = [["tmpf0", "tmpf0v", "tmpf1", "tmpf1v"], ["tmpf2", "tmpf3a", "tmpf3b"]]
        for blk in range(12):
            st = stage[blk % 2]
            c0 = blk * 512
            P.dma(POOL, st, modw_v[:, :, c0:c0 + 512], writes=skeys[blk % 2])
            pp = psB[blk % 2]

            def mm(e, st=st, pp=pp):
                for j in range(8):
                    i = e.matmul(pp, lhsT=csilbc[:, j, :], rhs=st[:, j, :], start=(j == 0), stop=(j == 7))
                return i
            T(mm, ["X:csilbc"] + skeys[blk % 2], [f"psB{blk % 2}"])
            V(lambda e, c0=c0, pp=pp: e.tensor_tensor(out=modbc[:, c0:c0 + 512], in0=pp, in1=modbc[:, c0:c0 + 512], op=ALU.add),
              [f"psB{blk % 2}", f"modbc{blk}"], [f"modbc{blk}"])
        for (slot, nm) in ((1, "norm_mix"), (4, "norm_ffn")):
            P.dma(SP, tmpf[0], D[nm][l:l + 1, :].partition_broadcast(128), writes=["tmpf0", "tmpf0v"])
            sl = modbc[:, slot * 1024:(slot + 1) * 1024]
            ks = [f"modbc{2 * slot}", f"modbc{2 * slot + 1}"]
            V(lambda e, sl=sl: e.scalar_tensor_tensor(out=sl, in0=sl, scalar=1.0, in1=tmpf[0], op0=ALU.add, op1=ALU.mult),
              ks + ["tmpf0", "tmpf0v"], ks)

    def modv(i):
        return modbc[:, i * 1024:(i + 1) * 1024], [f"modbc{2 * i}", f"modbc{2 * i + 1}"]

    def ada_tile(t, gslot, sslot, out_bf, out_bf_key, out_f32=None, out_f32_key=None):
        Gv, Gk = modv(gslot)
        Sv, Sk = modv(sslot)
        junk = tmpf[3][:, 0:512].bitcast(BF16)
        A(lambda e: e.activation(out=junk, in_=xres[:, t, :], func=AF.Square, accum_out=ssq[:, t:t + 1]), [f"x{t}"], ["tmpf3a", "tmpf3b", f"ssq{t}"])
        A(lambda e: e.activation(out=std[:, t:t + 1], in_=ssq[:, t:t + 1], func=AF.Sqrt, scale=1.0 / 1024, bias=EPS), [f"ssq{t}"], [f"std{t}"])
        V(lambda e: e.reciprocal(out=rstd[:, t:t + 1], in_=std[:, t:t + 1]), [f"std{t}"], [f"rstd{t}"])
        tf = tmpf[t % 2]
        tk = f"tmpf{t % 2}"
        V(lambda e: e.scalar_tensor_tensor(out=tf, in0=xres[:, t, :], scalar=rstd[:, t:t + 1], in1=Gv, op0=ALU.mult, op1=ALU.mult),
          [f"x{t}", f"rstd{t}"] + Gk, [tk, tk + "v"])
        if out_f32 is not None:
            V(lambda e: e.tensor_tensor(out=out_f32, in0=tf, in1=Sv, op=ALU.add), [tk] + Sk, [out_f32_key])
            A(lambda e: e.copy(out=out_bf, in_=out_f32), [out_f32_key], [out_bf_key])
        else:
            V(lambda e: e.tensor_tensor(out=out_bf, in0=tf, in1=Sv, op=ALU.add), [tk] + Sk, [out_bf_key])

    def transpose_tile_to_hT(src_bf, src_key, t, pt, ptk, dst=None, dkey=None):
        def tr(e):
            for j in range(8):
                i = e.transpose(out=pt[:, j * 128:(j + 1) * 128], in_=src_bf[:, j * 128:(j + 1) * 128], identity=identb[:])
            return i
        T(tr, [src_key, "identb"], [ptk])
        d = hT if dst is None else dst
        A(lambda e: e.copy(out=d[:, :, t * 128:(t + 1) * 128], in_=pt[:, :].rearrange("p (j n) -> p j n", j=8)),
          [ptk], [(dkey or "hT") + f"_{t}"])

    def mixer_norm_phase(l):
        for t in range(8):
            hb = hbuf[t % 2]
            hk = hbufk[t % 2]
            ada_tile(t, 1, 0, hb, hk)
            pt = (psT0, psT1)[t % 2]
            transpose_tile_to_hT(hb, hk, t, pt, f"psT{t % 2}")

    wcount = [0]

    def proj_block(wdram, col0, ncols, evac, hkey="hT"):
        i = wcount[0] % 2
        wcount[0] += 1
        wb = wblk[i][:, :, 0:ncols]
        wk = f"W:wblk{i}"
        P.dma(POOL, wb, wdram.rearrange("(j d) n -> d j n", d=128)[:, :, col0:col0 + ncols], writes=[wk])
        for t in range(8):
            pp = psB[t % 2]
            pk = f"psB{t % 2}"

            def mm(e, t=t, pp=pp):
                for j in range(8):
                    ins = e.matmul(pp[:, 0:ncols], lhsT=hT[:, j, t * 128:(t + 1) * 128], rhs=wb[:, j, :], start=(j == 0), stop=(j == 7))
                return ins
            T(mm, [wk, f"{hkey}_{t}"], [pk])
            evac(t, pp, pk)

    def head_transposes(src_bf, src_key, H, dstT, dst_key, h0, t, tok_off=0):
        pt = (psT0, psT1)[t % 2]
        ptk = f"psT{t % 2}"

        def tr(e):
            for h in range(H):
                i = e.transpose(out=pt[0:64, h * 128:(h + 1) * 128], in_=src_bf[:, h, :], identity=identb[:])
            return i
        T(tr, [src_key, "identb"], [ptk])
        c0 = tok_off + t * 128
        A(lambda e: e.copy(out=dstT[:, h0:h0 + H, c0:c0 + 128], in_=pt[0:64, 0:H * 128].rearrange("p (h n) -> p h n", h=H)),
          [ptk], [f"{dst_key}_{t}"])

    def norm_heads(pp, pk, H, gain, gkey, t, dst_f32, dkey):
        sq = tmpf[2][:, 0:H * 64]
        A(lambda e: e.activation(out=sq, in_=pp[:, 0:H * 64], func=AF.Square), [pk], ["tmpf2"])
        V(lambda e: e.tensor_reduce(out=hss[:, 0:H], in_=sq.rearrange("p (h d) -> p h d", d=64), axis=AX.X, op=ALU.add), ["tmpf2"], ["hss"])
        A(lambda e: e.activation(out=hstd[:, 0:H], in_=hss[:, 0:H], func=AF.Sqrt, scale=1.0 / 64, bias=EPS), ["hss"], ["hstd"])
        V(lambda e: e.reciprocal(out=hrs[:, 0:H], in_=hstd[:, 0:H]), ["hstd"], ["hrs"])
        V(lambda e: e.tensor_tensor(out=dst_f32, in0=pp[:, 0:H * 64].rearrange("p (h d) -> p h d", d=64),
                                    in1=hrs[:, 0:H].unsqueeze(2).to_broadcast([128, H, 64]), op=ALU.mult), [pk, "hrs"], [dkey])
        V(lambda e: e.tensor_tensor(out=dst_f32, in0=dst_f32, in1=gain.rearrange("p (h d) -> p h d", d=64), op=ALU.mult), [dkey, gkey], [dkey])

    def rope(src, skey, H, t, dst_bf, dkey, ctab=None, stab=None):
        ctab = cos_t if ctab is None else ctab
        stab = sin_t if stab is None else stab
        cs = ctab[:, t, :].unsqueeze(1).to_broadcast([128, H, 32])
        sn = stab[:, t, :].unsqueeze(1).to_broadcast([128, H, 32])
        x1 = src[:, :, 0:32]
        x2 = src[:, :, 32:64]
        t1 = tmpf[3][:, 0:H * 32].rearrange("p (h d) -> p h d", d=32)
        t2 = tmpf[3][:, 512:512 + H * 32].rearrange("p (h d) -> p h d", d=32)
        V(lambda e: e.tensor_tensor(out=t1, in0=x1, in1=cs, op=ALU.mult), [skey, "cos", "cos8"], ["tmpf3a"])
        V(lambda e: e.tensor_tensor(out=t2, in0=x2, in1=sn, op=ALU.mult), [skey, "sin", "sin8"], ["tmpf3b"])
        V(lambda e: e.tensor_tensor(out=dst_bf[:, :, 0:32], in0=t1, in1=t2, op=ALU.subtract), ["tmpf3a", "tmpf3b"], [dkey])
        V(lambda e: e.tensor_tensor(out=t1, in0=x1, in1=sn, op=ALU.mult), [skey, "sin"], ["tmpf3a"])
        V(lambda e: e.tensor_tensor(out=t2, in0=x2, in1=cs, op=ALU.mult), [skey, "cos"], ["tmpf3b"])
        V(lambda e: e.tensor_tensor(out=dst_bf[:, :, 32:64], in0=t1, in1=t2, op=ALU.add), ["tmpf3a", "tmpf3b"], [dkey])

    def bfstage(t, n):
        return PTb[t % 2][:, 0:n], f"PT{t % 2}"

    def load_cache_k(dram, nh, dstT, dkey, dv_=64):
        for kb in range(2):
            st = tmpf[2][:, 0:nh * 64].rearrange("p (h d) -> p h d", d=64)
            P.dma(SP, st, dram[:, kb * 128:(kb + 1) * 128, :].rearrange("h k d -> k h d"), writes=["tmpf2"])
            sbf, sk = bfstage(kb, nh * 64)
            sbf3 = sbf.rearrange("p (h d) -> p h d", d=64)
            A(lambda e, st=st, sbf3=sbf3: e.copy(out=sbf3, in_=st), ["tmpf2"], [sk])
            head_transposes(sbf3, sk, nh, dstT, dkey + "c", 0, kb, tok_off=0)

    def load_cache_v(dram, nh, dv, vaug, vkey):
        for kb in range(2):
            st = tmpf[2][:, 0:nh * dv].rearrange("p (h d) -> p h d", d=dv)
            P.dma(SP, st, dram[:, kb * 128:(kb + 1) * 128, :].rearrange("h k d -> k h d"), writes=["tmpf2"])
            A(lambda e, st=st, kb=kb: e.copy(out=vaug[:, kb, :, 0:dv], in_=st), ["tmpf2"], [f"{vkey}_{kb}"])

    def attention(nq, nkv, kv_of, v_of, qkey, kkey, vaug, vkey, dv, mvname, tri, finish, transposed):
        V(lambda e: e.memset(small[:, 392:394], 0.0), [], ["psO", "psOa", "psOb"])
        P.dma(SP, QTa[64:74, 0:nq, :], D[mvname].unsqueeze(1).to_broadcast([10, nq, 1024]), writes=["Q:mrows"])
        P.dma(SP, KTa[64:74, 0:nkv, :], D["selK"].unsqueeze(1).to_broadcast([10, nkv, 1280]), writes=["K:srows"])
        its = [(h, qh, kb) for h in range(nq) for qh in range(2) for kb in range(10)]

        def emit_scores(i):
            h, qh, kb = its[i]
            g = kv_of(h)
            pS = psS[i % 2]
            psk = f"psS{i % 2}"
            kkeys = [f"{kkey}c_{kb}"] if kb < 2 else [f"{kkey}_{kb - 2}"]
            qkeys = [f"{qkey}_{qh * 4 + j}" for j in range(4)]
            tris = []
            if tri and kb >= 2:
                for qb in range(4):
                    n = qh * 4 + qb
                    if kb - 2 == n - 1:
                        tris.append((qb, triA))
                    elif kb - 2 == n + 1:
                        tris.append((qb, triB))

            def mm(e):
                ins = e.matmul(pS, lhsT=KTa[:, g, kb * 128:(kb + 1) * 128], rhs=QTa[:, h, qh * 512:(qh + 1) * 512], start=True, stop=(len(tris) == 0))
                for n_, (qb, tr_) in enumerate(tris):
                    ins = e.matmul(pS[:, qb * 128:(qb + 1) * 128], lhsT=identb[:], rhs=tr_[:], start=False, stop=(n_ == len(tris) - 1))
                return ins
            T(mm, ["Q:mrows", "K:srows", "identb", "triA", "triB"] + kkeys + qkeys, [psk])
            pt = PTb[i % 2]
            A(lambda e: e.activation(out=pt, in_=pS, func=AF.Exp), [psk], [f"PT{i % 2}"])

        if transposed:
            accT = psA[0:dv + 1, 1024:1536]
            oT = tmpf[3][0:dv + 1, 0:512]
            pTr4 = psA[:, 1536:2048]

        def emit_pv(i):
            h, qh, kb = its[i]
            gv = v_of(h)
            pt = PTb[i % 2]
            if transposed:
                T(lambda e: e.matmul(accT, lhsT=vaug[:, kb, gv, 0:dv + 1], rhs=pt, start=(kb == 0), stop=(kb == 9)),
                  [f"PT{i % 2}", f"{vkey}_{kb}"], ["psOa"])
                if kb == 9:
                    A(lambda e: e.copy(out=oT, in_=accT), ["psOa"], ["tmpf3a", "tmpf3b"])

                    def tr(e):
                        for qb in range(4):
                            ins = e.transpose(out=pTr4[:, qb * (dv + 1):(qb + 1) * (dv + 1)], in_=oT[:, qb * 128:(qb + 1) * 128], identity=identf[0:dv + 1, 0:dv + 1])
                        return ins
                    T(tr, ["tmpf3a", "tmpf3b", "identf"], ["psOb"])
                return

            def pv(e):
                for qb in range(4):
                    ins = e.matmul(accO[qb][:, 0:dv + 1], lhsT=pt[:, qb * 128:(qb + 1) * 128],
                                   rhs=vaug[:, kb, gv, 0:dv + 1], start=(kb == 0), stop=(kb == 9))
                return ins
            T(pv, [f"PT{i % 2}", f"{vkey}_{kb}"], ACCK)

        emit_scores(0)
        for i in range(len(its)):
            if i + 1 < len(its):
                emit_scores(i + 1)
            emit_pv(i)
            if its[i][2] == 9:
                finish(its[i][0], its[i][1])
        V(lambda e: e.memset(small[:, 392:394], 0.0), [], ["psO", "psOa", "psOb"])

    def layer0():
        compute_mod(0)
        mixer_norm_phase(0)
        P.dma(SP, gk[:, 0:64], D["a_q_norm"].partition_broadcast(128), writes=["gk"])
        V(lambda e: e.tensor_scalar(out=gq[:].rearrange("p (h d) -> p h d", d=64), in0=gk[:, 0:64].unsqueeze(1).to_broadcast([128, 8, 64]),
                                    scalar1=0.125, scalar2=None, op0=ALU.mult), ["gk"], ["gq"])
        gk2 = sb("gk2", [128, 128])
        P.dma(SP, gk2[:, 0:64], D["a_k_norm"].partition_broadcast(128), writes=["gk2"])
        V(lambda e: e.tensor_copy(out=gk2[:, 64:128], in_=gk2[:, 0:64]), ["gk2"], ["gk2"])

        vaugA = R_V[:, 0:10 * 2 * 65].rearrange("p (k g d) -> p k g d", k=10, g=2)
        V(lambda e: e.memset(vaugA[:, :, :, 64:65], 1.0), [], [f"V:va_{kb}" for kb in range(10)])
        load_cache_k(D["cak"], 2, KT, "K:ka")
        load_cache_v(D["cav"], 2, 64, vaugA, "V:va")

        def evac_q(t, pp, pk):
            qn = tmpf[t % 2][:, 0:512].rearrange("p (h d) -> p h d", d=64)
            qk = f"tmpf{t % 2}"
            norm_heads(pp, pk, 8, gq[:], "gq", t, qn, qk)
            sbf, sk = bfstage(t, 512)
            sbf3 = sbf.rearrange("p (h d) -> p h d", d=64)
            rope(qn, qk, 8, t, sbf3, sk)
            head_transposes(sbf3, sk, 8, QT, "Q:qa", 0, t)
        proj_block(D["ev_w_in"], 0, 512, evac_q)

        def evac_kv(t, pp, pk):
            kn = tmpf[t % 2][:, 0:128].rearrange("p (h d) -> p h d", d=64)
            kk = f"tmpf{t % 2}"
            norm_heads(pp, pk, 2, gk2[:], "gk2", t, kn, kk)
            P.dma(SP, D["nak"][:, t * 128:(t + 1) * 128, :].rearrange("h k d -> k h d"), kn, reads=[kk], is_output=True)
            sbf, sk = bfstage(t, 128)
            sbf3 = sbf.rearrange("p (h d) -> p h d", d=64)
            rope(kn, kk, 2, t, sbf3, sk)
            head_transposes(sbf3, sk, 2, KT, "K:ka", 0, t, tok_off=256)
            vf = tmpf[t % 2][:, 512:640].rearrange("p (h d) -> p h d", d=64)
            vfk = f"tmpf{t % 2}v"
            A(lambda e: e.copy(out=vf, in_=pp[:, 128:256].rearrange("p (h d) -> p h d", d=64)), [pk], [vfk])
            P.dma(SP, D["nav"][:, t * 128:(t + 1) * 128, :].rearrange("h k d -> k h d"), vf, reads=[vfk], is_output=True)
            V(lambda e: e.tensor_copy(out=vaugA[:, 2 + t, :, 0:64], in_=vf), [vfk], [f"V:va_{2 + t}"])
        proj_block(D["ev_w_in"], 512, 256, evac_kv)

        def finA(h, qh):
            o4 = pO4[:, 0:260].rearrange("p (q d) -> p q d", q=4)
            V(lambda e: e.reciprocal(out=rdent[:, 0:4], in_=o4[:, :, 64]), ["psOb"], ["rdent"])
            V(lambda e: e.tensor_tensor(out=mixv[:, qh * 4:(qh + 1) * 4, h * 64:(h + 1) * 64], in0=o4[:, :, 0:64],
                                        in1=rdent[:, 0:4].unsqueeze(2).to_broadcast([128, 4, 64]), op=ALU.mult),
              ["psOb", "rdent"], [f"X:mix_{qh * 4 + j}" for j in range(4)])
        attention(8, 2, lambda h: h // 4, lambda h: h // 4, "Q:qa", "K:ka", vaugA, "V:va", 64, "mvA", False, finA, True)
        P.retire("Q", "K", "V", "M", "G", "W")

        P.dma(SP, dec, D["decay"].partition_broadcast(128), writes=["dec"])
        A(lambda e: e.activation(out=lgt, in_=dec, func=AF.Exp, scale=-1.0), ["dec"], ["lgt"])
        A(lambda e: e.activation(out=lgt, in_=lgt, func=AF.Ln, bias=1.0), ["lgt"], ["lgt"])
        V(lambda e: e.tensor_scalar(out=nlg, in0=lgt, scalar1=-1.0, scalar2=None, op0=ALU.mult), ["lgt"], ["nlg"])
        A(lambda e: e.activation(out=qdf, in_=nlg[:, 0:8], func=AF.Exp, scale=cols[:, 0:1]), ["nlg", "cols"], ["qdf"])
        A(lambda e: e.activation(out=qdb, in_=nlg[:, 8:16], func=AF.Exp, scale=cols[:, 1:2]), ["nlg", "cols"], ["qdb"])
        A(lambda e: e.activation(out=kdf, in_=nlg[:, 0:8], func=AF.Exp, scale=cols[:, 2:3]), ["nlg", "cols"], ["kdf"])
        A(lambda e: e.activation(out=kdb, in_=nlg[:, 8:16], func=AF.Exp, scale=cols[:, 3:4]), ["nlg", "cols"], ["kdb"])
        A(lambda e: e.activation(out=cdf, in_=nlg[:, 0:8], func=AF.Exp, scale=128.0), ["nlg"], ["cdf"])
        A(lambda e: e.activation(out=cdb, in_=nlg[:, 8:16], func=AF.Exp, scale=128.0), ["nlg"], ["cdb"])
        Dtot = R_S[:, 0:1024].rearrange("p (h n) -> p h n", h=8)
        dif = tmpf[2][:, 0:128]
        kpf = tmpf[2][:, 128:256]
        kpb = tmpf[2][:, 256:384]
        P.dma(SP, dif, D["diff"], writes=["tmpf2"])
        P.dma(SP, kpf, D["keepf"], writes=["tmpf2"])
        P.dma(SP, kpb, D["keepb"], writes=["tmpf2"])
        for h in range(8):
            e1 = tmpf[3][:, 0:128]
            e2 = tmpf[3][:, 512:640]
            A(lambda e, h=h: e.activation(out=e1, in_=dif, func=AF.Exp, scale=nlg[:, h:h + 1]), ["tmpf2", "nlg"], ["tmpf3a"])
            A(lambda e, h=h: e.activation(out=e2, in_=dif, func=AF.Exp, scale=lgt[:, 8 + h:9 + h]), ["tmpf2", "lgt"], ["tmpf3b"])
            V(lambda e: e.tensor_tensor(out=e1, in0=e1, in1=kpf, op=ALU.mult), ["tmpf3a", "tmpf2"], ["tmpf3a"])
            V(lambda e: e.tensor_tensor(out=e2, in0=e2, in1=kpb, op=ALU.mult), ["tmpf3b", "tmpf2"], ["tmpf3b"])
            V(lambda e, h=h: e.tensor_tensor(out=Dtot[:, h, :], in0=e1, in1=e2, op=ALU.add), ["tmpf3a", "tmpf3b"], ["S:Dtot"])

        def evac_rq(t, pp, pk):
            sbf, sk = bfstage(t, 512)
            A(lambda e: e.copy(out=sbf, in_=pp[:, 0:512]), [pk], [sk])
            head_transposes(sbf.rearrange("p (h d) -> p h d", d=64), sk, 8, QT, "Q:qr", 0, t)
        proj_block(D["ev_w_in"], 768, 512, evac_rq)
        Ktm = R_M[:, :].rearrange("p (c h d) -> p c h d", c=8, h=8)

        def evac_rk(t, pp, pk):
            sk = f"M:Ktm_{t}"
            A(lambda e: e.activation(out=Ktm[:, t, :, :], in_=pp[:, 0:512].rearrange("p (h d) -> p h d", d=64), func=AF.Copy, scale=0.125), [pk], [sk])
            head_transposes(Ktm[:, t, :, :], sk, 8, KT, "K:kr", 0, t)
        proj_block(D["ev_w_in"], 1280, 512, evac_rk)
        Vr = R_V[:, 0:4096].rearrange("p (c n) -> p c n", c=8)

        def evac_rv(t, pp, pk):
            A(lambda e: e.copy(out=Vr[:, t, :], in_=pp[:, 0:512]), [pk], [f"V:Vr_{t}"])
        proj_block(D["ev_w_in"], 1792, 512, evac_rv)
        gsil = R_G[:, :].rearrange("p (c n) -> p c n", c=8)

        def evac_rg(t, pp, pk):
            A(lambda e: e.activation(out=gsil[:, t, :], in_=pp[:, 0:512], func=AF.Silu), [pk], [f"G:gsil_{t}"])
        proj_block(D["ev_w_in"], 2304, 512, evac_rg)

        Ebf = R_W[0:64, 0:4096].rearrange("p (c h d) -> p c h d", c=8, h=8)
        Ebb = R_W[0:64, 4096:8192].rearrange("p (c h d) -> p c h d", c=8, h=8)
        wkeys = []
        P.retire("W")
        Ef = [tmpf[i][0:64, 0:512].rearrange("p (h d) -> p h d", h=8) for i in range(4)]
        Efk = [["tmpf0"], ["tmpf1"], ["tmpf2"], ["tmpf3a", "tmpf3b"]]
        pKV = psB[0][0:64, :].rearrange("p (h d) -> p h d", h=8)
        for (dirn, kdv, kdname, cd, s0name, Eb, ebname, outname, order, flo) in (
                ("f", kdf, "kdf", cdf, "s0f", Ebf, "W:Ebf", "nsf", list(range(8)), 0),
                ("b", kdb, "kdb", cdb, "s0b", Ebb, "W:Ebb", "nsb", list(range(7, -1, -1)), 8)):
            Ecur = Ef[0]
            P.dma(SP, Ecur, D[s0name].rearrange("h d v -> d h v"), writes=Efk[0])
            ecur_k = Efk[0][0]
            c0 = order[0]
            A(lambda e, Eb=Eb, c0=c0, Ecur=Ecur: e.copy(out=Eb[:, c0, :, :], in_=Ecur), [ecur_k], wkeys + [f"{ebname}_{c0}"])
            for n_, c in enumerate(order):
                Kd = PTb[n_ % 2].rearrange("p (h d) -> p h d", d=64)
                kdk = f"PT{n_ % 2}"
                V(lambda e, Kd=Kd, c=c, kdv=kdv: e.tensor_tensor(out=Kd, in0=Ktm[:, c, :, :], in1=kdv.unsqueeze(2).to_broadcast([128, 8, 64]), op=ALU.mult),
                  [f"M:Ktm_{c}", kdname], [kdk])

                def mm(e, Kd=Kd, c=c):
                    for h in range(8):
                        ins = e.matmul(pKV[:, h, :], lhsT=Kd[:, h, :], rhs=Vr[:, c, h * 64:(h + 1) * 64], start=True, stop=True)
                    return ins
                T(mm, [kdk, f"V:Vr_{c}"], ["psB0"])
                tmpE = Ef[1]
                V(lambda e, Ecur=Ecur, cd=cd: e.tensor_tensor(out=tmpE, in0=Ecur, in1=cd[0:64, :].unsqueeze(2).to_broadcast([64, 8, 64]), op=ALU.mult),
                  [ecur_k, "cd" + dirn], Efk[1])
                aft = Ef[2 + (n_ % 2)]
                aft_ks = Efk[2 + (n_ % 2)]
                aft_k = aft_ks[0]
                V(lambda e, aft=aft: e.tensor_tensor(out=aft, in0=tmpE, in1=pKV, op=ALU.add), Efk[1] + ["psB0"], aft_ks)
                is_out = (c % 2 == 1) if dirn == "f" else (c % 2 == 0)
                if is_out:
                    P.dma(SP, D[outname][c // 2].rearrange("h d v -> d h v"), aft, reads=[aft_k], is_output=True)
                if n_ < 7:
                    cn = order[n_ + 1]
                    V(lambda e, aft=aft, cn=cn, flo=flo: e.tensor_scalar(out=Ef[0], in0=aft, scalar1=flg[0:64, flo + cn:flo + cn + 1], scalar2=None, op0=ALU.mult),
                      [aft_k, "flg"], Efk[0])
                    Ecur = Ef[0]
                    ecur_k = Efk[0][0]
                    A(lambda e, Eb=Eb, cn=cn: e.copy(out=Eb[:, cn, :, :], in_=Ef[0]), Efk[0], [f"{ebname}_{cn}"])

        STD = [PTb[0].rearrange("p (h n) -> p h n", h=4), PTb[1].rearrange("p (h n) -> p h n", h=4)]
        pOI = psO[:, 0:512]
        pQf = psO[:, 512:1024]
        pQb = psB[1]
        for c in range(8):
            cs = slice(c * 128, (c + 1) * 128)
            for grp in range(2):
                pS = psS[grp]

                def mm(e, grp=grp, pS=pS, cs=cs):
                    for hh in range(4):
                        h = grp * 4 + hh
                        ins = e.matmul(pS[:, hh * 128:(hh + 1) * 128], lhsT=KT[:, h, cs], rhs=QT[:, h, cs], start=True, stop=True)
                    return ins
                T(mm, [f"K:kr_{c}", f"Q:qr_{c}"], [f"psS{grp}"])
                V(lambda e, grp=grp, pS=pS: e.tensor_tensor(out=STD[grp], in0=pS.rearrange("p (h n) -> p h n", h=4),
                                                            in1=Dtot[:, grp * 4:(grp + 1) * 4, :], op=ALU.mult), [f"psS{grp}", "S:Dtot"], [f"PT{grp}"])

            def mm2(e, c=c, cs=cs):
                for h in range(8):
                    e.matmul(pOI[:, h * 64:(h + 1) * 64], lhsT=STD[h // 4][:, h % 4, :], rhs=Vr[:, c, h * 64:(h + 1) * 64], start=True, stop=True)
                for h in range(8):
                    e.matmul(pQf[:, h * 64:(h + 1) * 64], lhsT=QT[:, h, cs], rhs=Ebf[:, c, h, :], start=True, stop=True)
                for h in range(8):
                    ins = e.matmul(pQb[:, h * 64:(h + 1) * 64], lhsT=QT[:, h, cs], rhs=Ebb[:, c, h, :], start=True, stop=True)
                return ins
            T(mm2, ["PT0", "PT1", f"V:Vr_{c}", f"Q:qr_{c}", f"W:Ebf_{c}", f"W:Ebb_{c}"], ["psO", "psB1"])
            o = tmpf[0][:, 0:512]
            t1 = tmpf[1][:, 0:512]
            t2 = tmpf[2][:, 0:512]
            o3 = o.rearrange("p (h d) -> p h d", d=64)
            V(lambda e: e.tensor_tensor(out=t1.rearrange("p (h d) -> p h d", d=64), in0=pQf.rearrange("p (h d) -> p h d", d=64),
                                        in1=qdf.unsqueeze(2).to_broadcast([128, 8, 64]), op=ALU.mult), ["psO", "qdf"], ["tmpf1"])
            V(lambda e: e.tensor_tensor(out=t2.rearrange("p (h d) -> p h d", d=64), in0=pQb.rearrange("p (h d) -> p h d", d=64),
                                        in1=qdb.unsqueeze(2).to_broadcast([128, 8, 64]), op=ALU.mult), ["psB1", "qdb"], ["tmpf2"])
            V(lambda e: e.tensor_tensor(out=o, in0=pOI, in1=t1, op=ALU.add), ["psO", "tmpf1"], ["tmpf0"])
            V(lambda e: e.tensor_tensor(out=o, in0=o, in1=t2, op=ALU.add), ["tmpf0", "tmpf2"], ["tmpf0"])
            V(lambda e: e.tensor_reduce(out=musum, in_=o3, axis=AX.X, op=ALU.add), ["tmpf0"], ["musum"])
            V(lambda e: e.tensor_scalar(out=negmu, in0=musum, scalar1=-1.0 / 64, scalar2=None, op0=ALU.mult), ["musum"], ["negmu"])
            V(lambda e: e.tensor_tensor(out=o3, in0=o3, in1=negmu.unsqueeze(2).to_broadcast([128, 8, 64]), op=ALU.add), ["tmpf0", "negmu"], ["tmpf0"])
            A(lambda e: e.activation(out=t1, in_=o, func=AF.Square), ["tmpf0"], ["tmpf1"])
            V(lambda e: e.tensor_reduce(out=varsum, in_=t1.rearrange("p (h d) -> p h d", d=64), axis=AX.X, op=ALU.add), ["tmpf1"], ["varsum"])
            A(lambda e: e.activation(out=vstd, in_=varsum, func=AF.Sqrt, scale=1.0 / 64, bias=EPS), ["varsum"], ["vstd"])
            V(lambda e: e.reciprocal(out=vrs, in_=vstd), ["vstd"], ["vrs"])
            V(lambda e: e.tensor_tensor(out=o3, in0=o3, in1=vrs.unsqueeze(2).to_broadcast([128, 8, 64]), op=ALU.mult), ["tmpf0", "vrs"], ["tmpf0"])
            V(lambda e, c=c: e.tensor_tensor(out=mixv[:, c, 512:1024], in0=o, in1=gsil[:, c, :], op=ALU.mult), ["tmpf0", f"G:gsil_{c}"], [f"X:mix_{c}"])

        P.retire("Q", "K", "V", "M", "G", "W")
        out_proj(D["ev_w_out"])

    def out_proj(wdram):
        for t in range(8):
            pt = (psT0, psT1)[t % 2]
            transpose_tile_to_hT(mixv[:, t, :], f"X:mix_{t}", t, pt, f"psT{t % 2}", dkey="hT")
        wv = wdram.rearrange("(j d) n -> d j n", d=128)
        for half in range(2):
            P.dma(POOL, wfull[:, :, half * 512:(half + 1) * 512], wv[:, :, half * 512:(half + 1) * 512],
                  reads=[f"hT_{t}" for t in range(8)], writes=["W:wblk0", "W:wblk1"])
        g2, g2k = modv(2)
        cnt = 0
        for t in range(8):
            for nb in range(2):
                pp = psB[cnt % 2]
                pk = f"psB{cnt % 2}"
                cnt += 1

                def mm(e, t=t, nb=nb, pp=pp):
                    for j in range(8):
                        ins = e.matmul(pp, lhsT=hT[:, j, t * 128:(t + 1) * 128], rhs=wfull[:, j, nb * 512:(nb + 1) * 512], start=(j == 0), stop=(j == 7))
                    return ins
                T(mm, ["W:wblk0", "W:wblk1", f"hT_{t}"], [pk])
                tt = tmpf[cnt % 2][:, 0:512]
                ttk = f"tmpf{cnt % 2}"
                V(lambda e, pp=pp, tt=tt, nb=nb: e.tensor_tensor(out=tt, in0=pp, in1=g2[:, nb * 512:(nb + 1) * 512], op=ALU.mult), [pk] + g2k, [ttk])
                V(lambda e, t=t, tt=tt, nb=nb: e.tensor_tensor(out=xres[:, t, nb * 512:(nb + 1) * 512], in0=xres[:, t, nb * 512:(nb + 1) * 512], in1=tt, op=ALU.add),
                  [ttk, f"x{t}"], [f"x{t}"])


    def layer1():
        lam_init = 0.8 - 0.6 * math.exp(-0.3 * 1)
        compute_mod(1)
        mixer_norm_phase(1)
        l4 = tmpf[2][:, 0:256].rearrange("p (a d) -> p a d", a=4)
        P.dma(SP, tmpf[2][:, 0:256], D["lam4"].rearrange("(o a) d -> o (a d)", o=1).partition_broadcast(128), writes=["tmpf2"])
        pr = tmpf[2][:, 256:384].rearrange("p (a d) -> p a d", a=2)
        V(lambda e: e.tensor_tensor(out=pr[:, 0, :], in0=l4[:, 0, :], in1=l4[:, 1, :], op=ALU.mult), ["tmpf2"], ["tmpf2"])
        V(lambda e: e.tensor_tensor(out=pr[:, 1, :], in0=l4[:, 2, :], in1=l4[:, 3, :], op=ALU.mult), ["tmpf2"], ["tmpf2"])
        V(lambda e: e.tensor_reduce(out=lamt[:, 0:2], in_=pr, axis=AX.X, op=ALU.add), ["tmpf2"], ["lamt"])
        A(lambda e: e.activation(out=lamt[:, 2:4], in_=lamt[:, 0:2], func=AF.Exp), ["lamt"], ["lamt"])
        V(lambda e: e.tensor_tensor(out=lamt[:, 4:5], in0=lamt[:, 3:4], in1=lamt[:, 2:3], op=ALU.subtract), ["lamt"], ["lamt"])
        V(lambda e: e.tensor_scalar(out=lamt[:, 5:6], in0=lamt[:, 4:5], scalar1=-lam_init, scalar2=None, op0=ALU.add), ["lamt"], ["neglam"])
        neglam = lamt[:, 5:6]
        sgain = gk[:, 0:128]
        P.dma(SP, sgain, D["subln"].partition_broadcast(128), writes=["gk"])
        V(lambda e: e.tensor_scalar(out=sgain, in0=sgain, scalar1=1.0 - lam_init, scalar2=None, op0=ALU.mult), ["gk"], ["gk"])
        P.dma(SP, sinkb, D["sink"].partition_broadcast(128), writes=["sinkb"])
        A(lambda e: e.activation(out=esink, in_=sinkb, func=AF.Exp), ["sinkb"], ["esink"])

        vaugC = R_V[:, 0:10 * 4 * 129].rearrange("p (k g d) -> p k g d", k=10, g=4)
        V(lambda e: e.memset(vaugC[:, :, :, 128:129], 1.0), [], [f"V:vc_{kb}" for kb in range(10)])
        load_cache_k(D["cck"], 8, KT, "K:kc")
        load_cache_v(D["ccv"], 4, 128, vaugC, "V:vc")

        def evac_cq(t, pp, pk):
            qf = tmpf[t % 2][:, 0:512].rearrange("p (h d) -> p h d", d=64)
            qk = f"tmpf{t % 2}"
            A(lambda e: e.activation(out=qf, in_=pp[:, 0:512].rearrange("p (h d) -> p h d", d=64), func=AF.Copy, scale=0.125), [pk], [qk])
            sbf, sk = bfstage(t, 512)
            sbf3 = sbf.rearrange("p (h d) -> p h d", d=64)
            rope(qf, qk, 8, t, sbf3, sk)
            head_transposes(sbf3, sk, 8, QT, "Q:qc", 0, t)
        proj_block(D["od_w_in"], 0, 512, evac_cq)

        def evac_ck(t, pp, pk):
            kf = tmpf[t % 2][:, 0:512].rearrange("p (h d) -> p h d", d=64)
            kk = f"tmpf{t % 2}"
            A(lambda e: e.copy(out=kf, in_=pp[:, 0:512].rearrange("p (h d) -> p h d", d=64)), [pk], [kk])
            P.dma(SP, D["nck"][:, t * 128:(t + 1) * 128, :].rearrange("h k d -> k h d"), kf, reads=[kk], is_output=True)
            sbf, sk = bfstage(t, 512)
            sbf3 = sbf.rearrange("p (h d) -> p h d", d=64)
            rope(kf, kk, 8, t, sbf3, sk)
            head_transposes(sbf3, sk, 8, KT, "K:kc", 0, t, tok_off=256)
        proj_block(D["od_w_in"], 512, 512, evac_ck)

        def evac_cv(t, pp, pk):
            vf = tmpf[t % 2][:, 0:512].rearrange("p (h d) -> p h d", d=128)
            vk = f"tmpf{t % 2}"
            A(lambda e: e.copy(out=vf, in_=pp[:, 0:512].rearrange("p (h d) -> p h d", d=128)), [pk], [vk])
            P.dma(SP, D["ncv"][:, t * 128:(t + 1) * 128, :].rearrange("h k d -> k h d"), vf, reads=[vk], is_output=True)
            V(lambda e: e.tensor_copy(out=vaugC[:, 2 + t, :, 0:128], in_=vf), [vk], [f"V:vc_{2 + t}"])
        proj_block(D["od_w_in"], 1024, 512, evac_cv)


        def finC(hc, qh):
            h = hc // 2
            o = tmpf[hc % 2][:, qh * 512:(qh + 1) * 512].rearrange("p (q d) -> p q d", q=4)
            ok = f"tmpf{hc % 2}" + ("v" if qh else "")
            for q in range(4):
                V(lambda e, q=q: e.reciprocal(out=rdent[:, q:q + 1], in_=accO[q][:, 128:129]), ACCK, [f"rdent{q}"])
                V(lambda e, q=q: e.tensor_scalar(out=o[:, q, :], in0=accO[q][:, 0:128], scalar1=rdent[:, q:q + 1], scalar2=None, op0=ALU.mult),
                  ACCK + [f"rdent{q}"], [ok])
            if hc % 2 == 1:
                o0 = tmpf[0][:, qh * 512:(qh + 1) * 512]
                o1 = tmpf[1][:, qh * 512:(qh + 1) * 512]
                k0 = "tmpf0" + ("v" if qh else "")
                k1 = "tmpf1" + ("v" if qh else "")
                V(lambda e: e.scalar_tensor_tensor(out=o0, in0=o1, scalar=neglam, in1=o0, op0=ALU.mult, op1=ALU.add), [k0, k1, "neglam"], [k0])
                sq = tmpf[2][:, 0:512]
                A(lambda e: e.activation(out=sq, in_=o0, func=AF.Square), [k0], ["tmpf2"])
                V(lambda e: e.tensor_reduce(out=hss[:, 0:4], in_=sq.rearrange("p (q d) -> p q d", q=4), axis=AX.X, op=ALU.add), ["tmpf2"], ["hss"])
                A(lambda e: e.activation(out=hstd[:, 0:4], in_=hss[:, 0:4], func=AF.Sqrt, scale=1.0 / 128, bias=EPS), ["hss"], ["hstd"])
                V(lambda e: e.reciprocal(out=hrs[:, 0:4], in_=hstd[:, 0:4]), ["hstd"], ["hrs"])
                o03 = o0.rearrange("p (q d) -> p q d", q=4)
                V(lambda e: e.tensor_tensor(out=o03, in0=o03, in1=hrs[:, 0:4].unsqueeze(2).to_broadcast([128, 4, 128]), op=ALU.mult), [k0, "hrs"], [k0])
                dst = mixv[:, qh * 4:(qh + 1) * 4, h * 128:(h + 1) * 128]
                V(lambda e: e.tensor_tensor(out=dst, in0=o03, in1=sgain.unsqueeze(1).to_broadcast([128, 4, 128]), op=ALU.mult),
                  [k0, "gk"], [f"X:mix_{qh * 4 + j}" for j in range(4)])
        attention(8, 8, lambda h: h, lambda h: h // 2, "Q:qc", "K:kc", vaugC, "V:vc", 128, "mvA", False, finC, False)
        P.retire("Q", "K", "V")

        vaugD = R_V[:, 0:10 * 2 * 65].rearrange("p (k g d) -> p k g d", k=10, g=2)
        V(lambda e: e.memset(vaugD[:, :, :, 64:65], 1.0), [], [f"V:vd_{kb}" for kb in range(10)])
        load_cache_k(D["cdk"], 2, KT, "K:kd")
        load_cache_v(D["cdv"], 2, 64, vaugD, "V:vd")

        def evac_dq(t, pp, pk):
            qf = tmpf[t % 2][:, 0:512].rearrange("p (h d) -> p h d", d=64)
            qk = f"tmpf{t % 2}"
            A(lambda e: e.activation(out=qf, in_=pp[:, 0:512].rearrange("p (h d) -> p h d", d=64), func=AF.Copy, scale=0.125), [pk], [qk])
            sbf, sk = bfstage(t, 512)
            sbf3 = sbf.rearrange("p (h d) -> p h d", d=64)
            rope(qf, qk, 8, t, sbf3, sk)
            head_transposes(sbf3, sk, 8, QT, "Q:qd", 0, t)
        proj_block(D["od_w_in"], 1536, 512, evac_dq)

        def evac_dkv(t, pp, pk):
            kvf = tmpf[t % 2][:, 0:256].rearrange("p (h d) -> p h d", d=64)
            kk = f"tmpf{t % 2}"
            A(lambda e: e.copy(out=kvf, in_=pp[:, 0:256].rearrange("p (h d) -> p h d", d=64)), [pk], [kk])
            P.dma(SP, D["ndk"][:, t * 128:(t + 1) * 128, :].rearrange("h k d -> k h d"), kvf[:, 0:2, :], reads=[kk], is_output=True)
            P.dma(SP, D["ndv"][:, t * 128:(t + 1) * 128, :].rearrange("h k d -> k h d"), kvf[:, 2:4, :], reads=[kk], is_output=True)
            sbf, sk = bfstage(t, 128)
            sbf3 = sbf.rearrange("p (h d) -> p h d", d=64)
            rope(kvf[:, 0:2, :], kk, 2, t, sbf3, sk)
            head_transposes(sbf3, sk, 2, KT, "K:kd", 0, t, tok_off=256)
            V(lambda e: e.tensor_copy(out=vaugD[:, 2 + t, :, 0:64], in_=kvf[:, 2:4, :]), [kk], [f"V:vd_{2 + t}"])
        proj_block(D["od_w_in"], 2048, 256, evac_dkv)

        def finD(h, qh):
            o4 = pO4[:, 0:260].rearrange("p (q d) -> p q d", q=4)
            V(lambda e: e.tensor_scalar(out=dent[:, 0:4], in0=o4[:, :, 64], scalar1=esink[:, h:h + 1], scalar2=None, op0=ALU.add), ["psOb", "esink"], ["dent"])
            V(lambda e: e.reciprocal(out=rdent[:, 0:4], in_=dent[:, 0:4]), ["dent"], ["rdent"])
            V(lambda e: e.tensor_tensor(out=mixv[:, qh * 4:(qh + 1) * 4, 512 + h * 64:512 + (h + 1) * 64], in0=o4[:, :, 0:64],
                                        in1=rdent[:, 0:4].unsqueeze(2).to_broadcast([128, 4, 64]), op=ALU.mult),
              ["psOb", "rdent"], [f"X:mix_{qh * 4 + j}" for j in range(4)])
        attention(8, 2, lambda h: h // 4, lambda h: h // 4, "Q:qd", "K:kd", vaugD, "V:vd", 64, "mvD", True, finD, True)
        if DEBUG:
            for t in range(8):
                P.dma(POOL, D["dbgm"][t * 128:(t + 1) * 128, :], mixv[:, t, :], reads=[f"X:mix_{t}"], is_output=True)
        P.retire("Q", "K", "V", "M", "G", "W")
        out_proj(D["od_w_out"])

    SHARED = ("Q", "K", "V", "M", "G", "W", "X", "S")

    def peer(l):
        P.retire(*SHARED)
        U_d = D[f"u{l}"]
        V_d = D[f"v{l}"]
        wqv = D["w_q"][l].rearrange("(j d) n -> d j n", d=128)
        wqA = R_Q[:, :].rearrange("p (j n) -> p j n", j=8)
        wqB = R_X[:, :].rearrange("p (j n) -> p j n", j=8)
        for half in range(2):
            P.dma(POOL, wqA[:, :, half * 512:(half + 1) * 512], wqv[:, :, half * 512:(half + 1) * 512], writes=["Q:wq"])
        for half in range(2):
            P.dma(POOL, wqB[:, :, half * 512:(half + 1) * 512], wqv[:, :, 1024 + half * 512:1024 + (half + 1) * 512], writes=["X:wq"])
        skf = R_T[:, 0:2048].rearrange("p (k d) -> p k d", k=16)
        P.dma(SP, skf, D["subkeys"][l].rearrange("k n d -> n k d"), writes=["tmpf0", "tmpf0v", "tmpf1", "tmpf1v"])
        skb = R_V[:, 2048:4096].rearrange("p (k d) -> p k d", k=16)
        A(lambda e: e.copy(out=skb, in_=skf), ["tmpf0", "tmpf1"], ["V:skb"])
        subkT = R_G[:, 2048:4096].rearrange("p (k n) -> p k n", k=16)
        for half in range(2):
            pt = (psT0, psT1)[half]

            def tr(e, half=half, pt=pt):
                for i in range(8):
                    ins = e.transpose(out=pt[:, i * 128:(i + 1) * 128], in_=skb[:, half * 8 + i, :], identity=identb[:])
                return ins
            T(tr, ["V:skb", "identb"], [f"psT{half}"])
            A(lambda e, half=half, pt=pt: e.copy(out=subkT[:, half * 8:(half + 1) * 8, :], in_=pt[:, :].rearrange("p (k n) -> p k n", k=8)),
              [f"psT{half}"], ["G:subkT"])

        idx_all = R_S[:, :].bitcast(I32).rearrange("p (t k) -> p t k", t=8)
        g_all = R_M[:, 0:2048].bitcast(F32).rearrange("p (t k) -> p t k", t=8)
        hf = R_M[:, 2048:4096].bitcast(F32)
        qTt = R_V[:, 0:2048].rearrange("p (k n) -> p k n", k=16)
        keys = R_T[:, 0:2048]
        keys2 = R_T[:, 2048:4096]
        KEYS = ["tmpf0", "tmpf0v", "tmpf1", "tmpf1v"]
        KEYS2 = ["tmpf2", "tmpf3a", "tmpf3b"]
        candr = R_K[:, 0:4096].bitcast(F32)
        cidxr = R_K[:, 4096:8192].bitcast(I32)
        top = R_K[:, 8192:8704].bitcast(F32).rearrange("p (k n) -> p k n", k=16)
        ii = R_K[:, 8704:9216].bitcast(I32).rearrange("p (k n) -> p k n", k=16)
        ctop = R_K[:, 9216:9472].bitcast(F32).rearrange("p (h n) -> p h n", h=8)
        egt = R_K[:, 9472:9728].bitcast(F32).rearrange("p (h n) -> p h n", h=8)
        negm = small[:, 400:408]
        zsum = small[:, 408:416]
        rz = small[:, 416:424]
        bankk = ["psS0", "psS1", "psO", "psO"]

        qTg = R_W[:, :].rearrange("p (k n) -> p k n", k=16)
        psAb = [psA[:, b * 512:(b + 1) * 512] for b in range(4)]
        for t in range(8):
            ada_tile(t, 4, 3, hbuf[t % 2], hbufk[t % 2])
            transpose_tile_to_hT(hbuf[t % 2], hbufk[t % 2], t, (psT0, psT1)[t % 2], f"psT{t % 2}")
        for t in range(8):
            grp, tt = t // 4, t % 4
            if tt == 0:
                for hc in range(16):
                    def mmq(e, hc=hc, grp=grp):
                        w = wqA if hc < 8 else wqB
                        c0 = (hc % 8) * 128
                        for j in range(8):
                            ins = e.matmul(psAb[hc % 4], lhsT=w[:, j, c0:c0 + 128], rhs=hT[:, j, grp * 512:(grp + 1) * 512],
                                           start=(j == 0), stop=(j == 7))
                        return ins
                    T(mmq, ["Q:wq", "X:wq"] + [f"hT_{grp * 4 + j}" for j in range(4)], [bankk[hc % 4]])
                    A(lambda e, hc=hc: e.copy(out=qTg[:, hc, :], in_=psAb[hc % 4]), [bankk[hc % 4]], [f"W:qTg_{hc}"])
            for b in range(4):
                def mms(e, b=b, tt=tt):
                    for hh in range(4):
                        hc = b * 4 + hh
                        ins = e.matmul(psA[:, hc * 128:(hc + 1) * 128], lhsT=qTg[:, hc, tt * 128:(tt + 1) * 128], rhs=subkT[:, hc, :], start=True, stop=True)
                    return ins
                T(mms, [f"W:qTg_{b * 4 + hh}" for hh in range(4)] + ["G:subkT"], [bankk[b]])
                A(lambda e, b=b: e.copy(out=keys[:, b * 512:(b + 1) * 512], in_=psA[:, b * 512:(b + 1) * 512]),
                  [bankk[b]], [KEYS[b]])
            ki = keys.bitcast(I32)
            V(lambda e: e.tensor_scalar(out=ki, in0=ki, scalar1=-128, scalar2=None, op0=ALU.bitwise_and), KEYS, KEYS)
            V(lambda e: e.tensor_tensor(out=ki.rearrange("p (k n) -> p k n", k=16), in0=ki.rearrange("p (k n) -> p k n", k=16),
                                        in1=iota[:, :].unsqueeze(1).to_broadcast([128, 16, 128]), op=ALU.bitwise_or), KEYS + ["iota"], KEYS)
            for hc in range(16):
                kv = keys[:, hc * 128:(hc + 1) * 128]
                V(lambda e, hc=hc, kv=kv: e.max(out=top[:, hc, 0:8], in_=kv), KEYS, [f"K:topa{hc}"])
            for hc in range(16):
                kv = keys[:, hc * 128:(hc + 1) * 128]
                kv2 = keys2[:, hc * 128:(hc + 1) * 128]
                V(lambda e, hc=hc, kv=kv, kv2=kv2: e.match_replace(out=kv2, in_to_replace=top[:, hc, 0:8], in_values=kv, imm_value=-1e30),
                  KEYS + [f"K:topa{hc}"], [f"k2_{hc}"] + (KEYS2 if hc == 0 else []))
            for hc in range(16):
                kv2 = keys2[:, hc * 128:(hc + 1) * 128]
                V(lambda e, hc=hc, kv2=kv2: e.max(out=top[:, hc, 8:16], in_=kv2), [f"k2_{hc}"] + KEYS2, [f"K:topb{hc}"])
            TOPK = [f"K:topa{hc}" for hc in range(16)] + [f"K:topb{hc}" for hc in range(16)]
            topi = top.bitcast(I32)
            V(lambda e: e.tensor_scalar(out=ii, in0=topi, scalar1=127, scalar2=None, op0=ALU.bitwise_and), TOPK, ["K:ii"])
            ii4 = ii.rearrange("p (h c) n -> p h c n", c=2)
            top4 = top.rearrange("p (h c) n -> p h c n", c=2)
            V(lambda e: e.tensor_scalar(out=ii4[:, :, 0, :], in0=ii4[:, :, 0, :], scalar1=7, scalar2=None, op0=ALU.logical_shift_left), ["K:ii"], ["K:ii"])
            cand4 = candr.rearrange("p (h a b) -> p h a b", h=8, a=16)
            cidx4 = cidxr.rearrange("p (h a b) -> p h a b", h=8, a=16)
            V(lambda e: e.tensor_tensor(out=cand4, in0=top4[:, :, 0, :].unsqueeze(3).to_broadcast([128, 8, 16, 16]),
                                        in1=top4[:, :, 1, :].unsqueeze(2).to_broadcast([128, 8, 16, 16]), op=ALU.add), TOPK, ["K:cand"])
            V(lambda e: e.tensor_tensor(out=cidx4, in0=ii4[:, :, 0, :].unsqueeze(3).to_broadcast([128, 8, 16, 16]),
                                        in1=ii4[:, :, 1, :].unsqueeze(2).to_broadcast([128, 8, 16, 16]), op=ALU.bitwise_or), ["K:ii"], ["K:cidx"])
            ci = candr.bitcast(I32)
            V(lambda e: e.tensor_scalar(out=ci, in0=ci, scalar1=-16384, scalar2=None, op0=ALU.bitwise_and), ["K:cand"], ["K:cand"])
            V(lambda e: e.tensor_tensor(out=ci, in0=ci, in1=cidxr, op=ALU.bitwise_or), ["K:cand", "K:cidx"], ["K:cand"])
            for h in range(8):
                cv = candr[:, h * 256:(h + 1) * 256]
                V(lambda e, h=h, cv=cv: e.max(out=ctop[:, h, 0:8], in_=cv), ["K:cand"], [f"K:ctopa{h}"])
            for h in range(8):
                cv = candr[:, h * 256:(h + 1) * 256]
                cv2 = keys[:, h * 256:(h + 1) * 256]
                V(lambda e, h=h, cv=cv, cv2=cv2: e.match_replace(out=cv2, in_to_replace=ctop[:, h, 0:8], in_values=cv, imm_value=-1e30),
                  ["K:cand", f"K:ctopa{h}"], [f"c2_{h}"] + (KEYS if h == 0 else []))
            for h in range(8):
                cv2 = keys[:, h * 256:(h + 1) * 256]
                V(lambda e, h=h, cv2=cv2: e.max(out=ctop[:, h, 8:16], in_=cv2), [f"c2_{h}"] + KEYS, [f"K:ctopb{h}"])
            CTOP = [f"K:ctopa{h}" for h in range(8)] + [f"K:ctopb{h}" for h in range(8)]
            V(lambda e, t=t: e.tensor_scalar(out=idx_all[:, t, :], in0=ctop.bitcast(I32).rearrange("p h n -> p (h n)"), scalar1=16383, scalar2=None, op0=ALU.bitwise_and),
              CTOP, [f"S:idx_{t}"])
            V(lambda e: e.tensor_scalar(out=negm, in0=ctop[:, :, 0], scalar1=-1.0, scalar2=None, op0=ALU.mult), CTOP, ["negm"])
            for h in range(8):
                A(lambda e, h=h: e.activation(out=egt[:, h, :], in_=ctop[:, h, :], func=AF.Exp, bias=negm[:, h:h + 1], accum_out=zsum[:, h:h + 1]),
                  CTOP + ["negm"], [f"K:egt{h}", f"zsum{h}"])
            V(lambda e: e.reciprocal(out=rz, in_=zsum), [f"zsum{h}" for h in range(8)], ["rz"])
            V(lambda e, t=t: e.tensor_tensor(out=g_all[:, t, :].rearrange("p (h n) -> p h n", h=8), in0=egt, in1=rz.unsqueeze(2).to_broadcast([128, 8, 16]), op=ALU.mult),
              [f"K:egt{h}" for h in range(8)] + ["rz"], [f"M:g_{t}"])

        P.retire("Q", "K", "V", "G", "W", "X")
        ring = []
        for (buf, n, nm) in ((R_K, 5, "K"), (R_W, 4, "W"), (R_V, 2, "V"), (R_Q, 4, "Q"), (R_X, 4, "X")):
            for i in range(n):
                ring.append((buf[:, i * 2048:(i + 1) * 2048].bitcast(F32), f"{nm}:ring{i}"))
        NR = len(ring)
        accs = [R_G[:, 0:2048].bitcast(F32), R_G[:, 2048:4096].bitcast(F32)]
        acck = ["G:acc0", "G:acc1"]
        Araw = small[:, 256:384]
        g2, g2k = modv(5)
        rc = [0]
        xx = R_P[:, 0:1024].bitcast(F32)
        x2 = xx[:, 0:128]; inner = xx[:, 128:256]; sg = xx[:, 256:384]; wgt = xx[:, 384:512]
        for t in range(8):
            ada_tile(t, 4, 3, hbuf[0], "hbuf0", out_f32=hf, out_f32_key="M:hf")
            for k in range(128):
                buf, bk = ring[rc[0] % NR]
                rc[0] += 1
                P.op(POOL, lambda e, buf=buf, t=t, k=k: e.indirect_dma_start(out=buf, out_offset=None, in_=U_d,
                                                                              in_offset=bass.IndirectOffsetOnAxis(ap=idx_all[:, t, k:k + 1], axis=0)),
                     [f"S:idx_{t}"], [bk], dma=True)
                V(lambda e, buf=buf, k=k: e.scalar_tensor_tensor(out=buf, in0=buf, scalar=1.0, in1=hf, op0=ALU.mult, op1=ALU.mult, accum_out=Araw[:, k:k + 1]),
                  [bk, "M:hf"], [bk, "Araw"])
            V(lambda e: e.memset(small[:, 384:392], 0.0), [], ["dummy"])
            V(lambda e: e.tensor_tensor(out=x2, in0=Araw, in1=Araw, op=ALU.mult), ["Araw"], ["PT0"])
            V(lambda e: e.tensor_scalar(out=x2, in0=x2, scalar1=0.044715, scalar2=1.0, op0=ALU.mult, op1=ALU.add), ["PT0"], ["PT0"])
            V(lambda e: e.tensor_tensor(out=inner, in0=x2, in1=Araw, op=ALU.mult), ["PT0", "Araw"], ["PT0"])
            A(lambda e: e.activation(out=sg, in_=inner, func=AF.Sigmoid, scale=1.5957691216057308), ["PT0"], ["PT1"])
            V(lambda e: e.tensor_tensor(out=wgt, in0=sg, in1=Araw, op=ALU.mult), ["PT1", "Araw"], ["PT1"])
            V(lambda e, t=t: e.tensor_tensor(out=wgt, in0=wgt, in1=g_all[:, t, :], op=ALU.mult), ["PT1", f"M:g_{t}"], ["PT1"])
            for k in range(128):
                buf, bk = ring[rc[0] % NR]
                rc[0] += 1
                P.op(POOL, lambda e, buf=buf, t=t, k=k: e.indirect_dma_start(out=buf, out_offset=None, in_=V_d,
                                                                              in_offset=bass.IndirectOffsetOnAxis(ap=idx_all[:, t, k:k + 1], axis=0)),
                     [f"S:idx_{t}"], [bk], dma=True)
                acc = accs[k % 2]
                ak = acck[k % 2]
                if k < 2:
                    V(lambda e, buf=buf, k=k, acc=acc: e.tensor_scalar(out=acc, in0=buf, scalar1=wgt[:, k:k + 1], scalar2=None, op0=ALU.mult), [bk, "PT1"], [ak])
                else:
                    V(lambda e, buf=buf, k=k, acc=acc: e.scalar_tensor_tensor(out=acc, in0=buf, scalar=wgt[:, k:k + 1], in1=acc, op0=ALU.mult, op1=ALU.add),
                      [bk, "PT1", ak], [ak])
            V(lambda e: e.tensor_tensor(out=accs[0], in0=accs[0], in1=accs[1], op=ALU.add), acck, [acck[0]])
            V(lambda e: e.tensor_tensor(out=accs[0], in0=accs[0], in1=g2, op=ALU.mult), [acck[0]] + g2k, [acck[0]])
            V(lambda e, t=t: e.tensor_tensor(out=xres[:, t, :], in0=xres[:, t, :], in1=accs[0], op=ALU.add), [acck[0], f"x{t}"], [f"x{t}"])
        P.retire(*SHARED)

    def final_norm():
        P.dma(SP, tmpf[2], D["norm_final"].partition_broadcast(128), writes=["tmpf2"])
        for t in range(8):
            junk = tmpf[3][:, 0:512].bitcast(BF16)
            A(lambda e, t=t: e.activation(out=junk, in_=xres[:, t, :], func=AF.Square, accum_out=ssq[:, t:t + 1]), [f"x{t}"], ["tmpf3a", "tmpf3b", f"ssq{t}"])
            A(lambda e, t=t: e.activation(out=std[:, t:t + 1], in_=ssq[:, t:t + 1], func=AF.Sqrt, scale=1.0 / 1024, bias=EPS), [f"ssq{t}"], [f"std{t}"])
            V(lambda e, t=t: e.reciprocal(out=rstd[:, t:t + 1], in_=std[:, t:t + 1]), [f"std{t}"], [f"rstd{t}"])
            tf = tmpf[t % 2]
            V(lambda e, t=t, tf=tf: e.scalar_tensor_tensor(out=tf, in0=xres[:, t, :], scalar=rstd[:, t:t + 1], in1=tmpf[2], op0=ALU.mult, op1=ALU.mult),
              [f"x{t}", f"rstd{t}", "tmpf2"], [f"tmpf{t % 2}", f"tmpf{t % 2}v"])
            P.dma(SP, D["y"][t * 128:(t + 1) * 128, :], tf, reads=[f"tmpf{t % 2}", f"tmpf{t % 2}v"], is_output=True)

    def zero_outputs(names):
        z = tmpf[3]
        V(lambda e: e.memset(z, 0.0), [], ["tmpf3a", "tmpf3b"])
        for n in names:
            flat = D[n]
            tot = 1
            for s in flat.shape:
                tot *= s
            nd = len(flat.shape)
            if nd == 3:
                v = flat.rearrange("a b c -> (a b c)")
            elif nd == 4:
                v = flat.rearrange("a b c d -> (a b c d)")
            else:
                v = flat.rearrange("a b -> (a b)")
            v = v.rearrange("(p n) -> p n", p=128)
            per = tot // 128
            for c0 in range(0, per, 1024):
                w_ = min(1024, per - c0)
                P.dma(SP, v[:, c0:c0 + w_], z[:, 0:w_], reads=["tmpf3a", "tmpf3b"], is_output=True)

    if STOP_AFTER == "l1only":
        zero_outputs(["nak", "nav", "nsf", "nsb"])
        layer1()
        dbg_dump(2)
        final_norm()
    else:
        layer0()
        dbg_dump(0)
    if STOP_AFTER == "l1only":
        pass
    elif STOP_AFTER == "mix0":
        zero_outputs(["nck", "ncv", "ndk", "ndv"])
        final_norm()
    elif STOP_AFTER == "peer0":
        peer(0)
        dbg_dump(1)
        zero_outputs(["nck", "ncv", "ndk", "ndv"])
        final_norm()
    elif STOP_AFTER == "mix1":
        layer1()
        dbg_dump(2)
        final_norm()
    else:
        peer(0)
        dbg_dump(1)
        layer1()
        dbg_dump(2)
        peer(1)
        dbg_dump(3)
        final_norm()
    counts = P.emit()
    if DEBUG:
        print("ops per engine:", counts)
    es.close()
    return nc


_CACHE = {}


def _host_constants(is_sample):
    bf = ml_dtypes.bfloat16
    c = {}
    if is_sample:
        rows = 1024 // 64
        row = np.repeat(np.arange(rows, dtype=np.float32), 64)
        col = np.tile(np.arange(64, dtype=np.float32), rows)
        inv = (np.float32(10000.0) ** (-np.arange(16, dtype=np.float32) / np.float32(16))).astype(np.float32)
        ang = np.concatenate([row[:, None] * inv, col[:, None] * inv], axis=-1).astype(np.float32)
        c["rope_cos"] = np.cos(ang).astype(np.float32)
        c["rope_sin"] = np.sin(ang).astype(np.float32)
        mvA = np.zeros((10, 1024), np.float32)
        mvD = np.zeros((10, 1024), np.float32)
        for kb in range(2, 10):
            for qb in range(8):
                if abs((kb - 2) - qb) > 1:
                    mvD[kb, qb * 128:(qb + 1) * 128] = NEG
        kj = np.arange(128)[:, None]
        qi = np.arange(128)[None, :]
        triA = np.where(kj >= qi, 0.0, NEG)
        triB = np.where(kj <= qi, 0.0, NEG)
        flags = np.ones((1, 16), np.float32)
    else:
        c["rope_cos"] = np.ones((1024, 32), np.float32)
        c["rope_sin"] = np.zeros((1024, 32), np.float32)
        mvA = np.zeros((10, 1024), np.float32)
        mvA[0:2, :] = NEG
        for kb in range(2, 10):
            for qb in range(8):
                if (kb - 2) // 2 != qb // 2:
                    mvA[kb, qb * 128:(qb + 1) * 128] = NEG
        mvD = mvA.copy()
        triA = np.zeros((128, 128), np.float32)
        triB = np.zeros((128, 128), np.float32)
        flags = np.zeros((1, 16), np.float32)
        for cidx in range(8):
            flags[0, cidx] = 1.0 if cidx % 2 == 1 else 0.0
            flags[0, 8 + cidx] = 1.0 if cidx % 2 == 0 else 0.0
    c["mvA"] = mvA.astype(bf)
    c["mvD"] = mvD.astype(bf)
    c["triA"] = triA.astype(bf)
    c["triB"] = triB.astype(bf)
    c["flags"] = flags
    sel = np.zeros((10, 10, 128), np.float32)
    for kb in range(10):
        sel[kb, kb, :] = 1.0
    c["sel"] = sel.astype(bf)
    selK = np.zeros((10, 1280), np.float32)
    for kb in range(10):
        selK[kb, kb * 128:(kb + 1) * 128] = 1.0
    c["selK"] = selK.astype(bf)
    c["identb"] = np.eye(128, dtype=np.float32).astype(bf)
    c["identf"] = np.eye(128, dtype=np.float32)
    r = np.arange(128, dtype=np.float32)[:, None]
    p = np.arange(128, dtype=np.float32)[None, :]
    c["diff"] = (p - r).astype(np.float32)
    c["keepf"] = (p >= r).astype(np.float32)
    c["keepb"] = (r > p).astype(np.float32)
    pp = np.arange(128, dtype=np.float32)
    c["cols"] = np.stack([pp + 1, 128 - pp, 127 - pp, pp], axis=1).astype(np.float32)
    c["iota128"] = np.broadcast_to(np.arange(128, dtype=np.int32)[None, :], (128, 128)).copy()
    return c


def kernel(x_prompt, x_sample, c, c_ctx, cache_a_k, cache_a_v, state_ret_fwd, state_ret_bwd,
           cache_c_k, cache_c_v, cache_d_k, cache_d_v, mod_w, mod_b, norm_mix, norm_ffn, norm_final,
           ev_w_in, ev_w_out, a_q_norm, a_k_norm, ret_decay_fwd, ret_decay_bwd,
           od_w_in, od_w_out, c_lambda_q1, c_lambda_k1, c_lambda_q2, c_lambda_k2, c_subln, d_sink,
           peer_w_q, peer_subkeys, peer_u, peer_v):
    f = lambda a: np.ascontiguousarray(np.asarray(a), dtype=np.float32)
    if "nc" not in _CACHE:
        _CACHE["nc"] = build_program()
    nc = _CACHE["nc"]
    shared = {
        "mod_w": f(mod_w), "mod_b": f(mod_b), "norm_mix": f(norm_mix), "norm_ffn": f(norm_ffn),
        "norm_final": f(norm_final).reshape(1, 1024),
        "ev_w_in": f(ev_w_in)[0], "ev_w_out": f(ev_w_out)[0],
        "a_q_norm": f(a_q_norm).reshape(1, 64), "a_k_norm": f(a_k_norm).reshape(1, 64),
        "decay": np.concatenate([f(ret_decay_fwd).reshape(1, 8), f(ret_decay_bwd).reshape(1, 8)], axis=1),
        "od_w_in": f(od_w_in)[0], "od_w_out": f(od_w_out)[0],
        "lam4": np.concatenate([f(c_lambda_q1), f(c_lambda_k1), f(c_lambda_q2), f(c_lambda_k2)], axis=0),
        "subln": f(c_subln).reshape(1, 128), "sink": f(d_sink).reshape(1, 8),
        "w_q": f(peer_w_q), "subkeys": f(peer_subkeys).reshape(2, 16, 128, 128),
        "u0": f(peer_u)[0], "u1": f(peer_u)[1], "v0": f(peer_v)[0], "v1": f(peer_v)[1],
    }
    xp = f(x_prompt)
    xs = f(x_sample)
    in_maps = []
    for ci in range(8):
        m = dict(shared)
        samp = ci < 4
        m.update(_host_constants(samp))
        if samp:
            b = ci
            m["x"] = xs[b]
            m["cvec"] = f(c)[b:b + 1]
            m["cak"] = f(cache_a_k)[b, 0]; m["cav"] = f(cache_a_v)[b, 0]
            m["s0f"] = f(state_ret_fwd)[b, 0]; m["s0b"] = f(state_ret_bwd)[b, 0]
            m["cck"] = f(cache_c_k)[b, 0]; m["ccv"] = f(cache_c_v)[b, 0]
            m["cdk"] = f(cache_d_k)[b, 0]; m["cdv"] = f(cache_d_v)[b, 0]
        else:
            g = ci - 4
            m["x"] = xp[4 * g:4 * g + 4].reshape(1024, 1024)
            m["cvec"] = f(c_ctx).reshape(1, 1024)
            m["cak"] = np.zeros((2, 256, 64), np.float32); m["cav"] = np.zeros((2, 256, 64), np.float32)
            m["s0f"] = np.zeros((8, 64, 64), np.float32); m["s0b"] = np.zeros((8, 64, 64), np.float32)
            m["cck"] = np.zeros((8, 256, 64), np.float32); m["ccv"] = np.zeros((4, 256, 128), np.float32)
            m["cdk"] = np.zeros((2, 256, 64), np.float32); m["cdv"] = np.zeros((2, 256, 64), np.float32)
        if SMALL_UV:
            for k in ("u0", "u1", "v0", "v1"):
                m[k] = m[k][:128]
        in_maps.append({k: np.ascontiguousarray(v) for k, v in m.items()})
    res = run_bass_kernel_spmd(nc, in_maps, core_ids=list(range(8)))
    R = res.results
    _CACHE["last"] = R
    y_sample = np.stack([R[ci]["y"] for ci in range(4)], axis=0)
    y_prompt = np.concatenate([R[ci]["y"].reshape(4, 256, 1024) for ci in range(4, 8)], axis=0)

    def kvout(name, nh, dv):
        parts = [R[ci][name].reshape(nh, 4, 256, dv).transpose(1, 0, 2, 3)[:, None] for ci in range(4, 8)]
        return np.ascontiguousarray(np.concatenate(parts, axis=0), dtype=np.float32)

    def stout(name):
        parts = [R[ci][name].reshape(4, 1, 8, 64, 64) for ci in range(4, 8)]
        return np.ascontiguousarray(np.concatenate(parts, axis=0), dtype=np.float32)

    return (y_prompt.astype(np.float32), y_sample.astype(np.float32),
            kvout("nak", 2, 64), kvout("nav", 2, 64), stout("nsf"), stout("nsb"),
            kvout("nck", 8, 64), kvout("ncv", 4, 128), kvout("ndk", 2, 64), kvout("ndv", 2, 64))
```

```python
import os
import math
import contextlib
import numpy as np
import ml_dtypes
import concourse.bass as bass
import concourse.mybir as mybir
from concourse.bass_utils import run_bass_kernel_spmd

F32 = mybir.dt.float32
BF16 = mybir.dt.bfloat16
I32 = mybir.dt.int32
AF = mybir.ActivationFunctionType
ALU = mybir.AluOpType
AX = mybir.AxisListType

PE, ACT, DVE, POOL, SP = "tensor", "scalar", "vector", "gpsimd", "sync"
ENGINES = (PE, ACT, DVE, POOL, SP)
SEM_CHUNK = 20000
DMA_RING = 16
EPS = 1e-6
NEG = -30000.0
DEBUG = bool(int(os.environ.get("MK_DEBUG", "0")))
STOP_AFTER = os.environ.get("MK_STOP", "")
STRICT = bool(int(os.environ.get("MK_STRICT", "1")))
SMALL_UV = STOP_AFTER in ("mix0", "mix1", "l1only")


class Op:
    __slots__ = ("eng", "fn", "is_dma", "deps", "signal", "idx", "sig_no", "dma_no")

    def __init__(self, eng, fn, is_dma):
        self.eng = eng
        self.fn = fn
        self.is_dma = is_dma
        self.deps = []
        self.signal = False
        self.sig_no = -1
        self.dma_no = -1


class Prog:
    def __init__(self, nc):
        self.nc = nc
        self.ops = []
        self.last_writer = {}
        self.readers = {}
        self.out_dmas = []
        self.group_last = {}
        self.group_dmas = {}
        self.fence = {}

    def retire(self, *groups):
        for g in groups:
            self.fence[g] = list(self.group_last.get(g, {}).values()) + self.group_dmas.get(g, [])
            self.group_last[g] = {}
            self.group_dmas[g] = []

    def op(self, eng, fn, reads=(), writes=(), dma=False):
        o = Op(eng, fn, dma)
        o.idx = len(self.ops)
        deps = {}
        groups = set()
        for k in list(reads) + list(writes):
            i = k.find(":")
            if i >= 0:
                groups.add(k[:i])
        for g in groups:
            for p in self.fence.get(g, ()):
                if p.idx not in deps:
                    deps[p.idx] = (p, False)
        for r in reads:
            w = self.last_writer.get(r)
            if w is not None:
                deps[w.idx] = (w, True)
        for k in writes:
            w = self.last_writer.get(k)
            if w is not None and w.idx not in deps:
                deps[w.idx] = (w, False)
            for rd in self.readers.get(k, ()):
                if rd.idx not in deps:
                    deps[rd.idx] = (rd, False)
        for (p, raw) in deps.values():
            if p.eng == o.eng and not p.is_dma and not o.is_dma:
                if o.eng == PE or (not raw and not STRICT):
                    continue
            o.deps.append(p)
        for r in reads:
            self.readers.setdefault(r, []).append(o)
        for k in writes:
            self.last_writer[k] = o
            self.readers[k] = []
        for g in groups:
            if dma:
                self.group_dmas.setdefault(g, []).append(o)
            else:
                self.group_last.setdefault(g, {})[eng] = o
        self.ops.append(o)
        return o

    def dma(self, eng, out, in_, reads=(), writes=(), is_output=False, **kw):
        o = self.op(eng, lambda e: e.dma_start(out=out, in_=in_, **kw), reads, writes, dma=True)
        if is_output:
            self.out_dmas.append(o)
        return o

    def emit(self):
        nc = self.nc
        for o in self.ops:
            for p in o.deps:
                p.signal = True
            if o.is_dma:
                o.signal = True
        n_sig = {e: 0 for e in ENGINES}
        n_dma = {e: 0 for e in ENGINES}
        for o in self.ops:
            if o.is_dma:
                o.dma_no = n_dma[o.eng]
                n_dma[o.eng] += 1
            elif o.signal:
                o.sig_no = n_sig[o.eng]
                n_sig[o.eng] += 1
        es = contextlib.ExitStack()
        sems, dsems = {}, {}
        for e in ENGINES:
            n = (n_sig[e] + SEM_CHUNK - 1) // SEM_CHUNK
            sems[e] = [es.enter_context(nc.semaphore(f"s_{e}_{i}")) for i in range(n)]
            if n_dma[e]:
                dsems[e] = [es.enter_context(nc.semaphore(f"d_{e}_{i}")) for i in range(min(DMA_RING, n_dma[e]))]

        def target(p):
            if p.is_dma:
                return dsems[p.eng][p.dma_no % DMA_RING], 16 * (p.dma_no // DMA_RING + 1)
            return sems[p.eng][p.sig_no // SEM_CHUNK], p.sig_no % SEM_CHUNK + 1

        per_eng = {e: [o for o in self.ops if o.eng == e] for e in ENGINES}
        final_waits = [target(o) for o in self.out_dmas]

        def run(eng_name, eng):
            seen = {}
            for o in per_eng[eng_name]:
                waits = {}
                for p in o.deps:
                    s, v = target(p)
                    if seen.get(s.name, 0) >= v:
                        continue
                    if waits.get(s.name, (None, 0))[1] < v:
                        waits[s.name] = (s, v)
                if o.is_dma and o.dma_no >= DMA_RING:
                    s = dsems[eng_name][o.dma_no % DMA_RING]
                    v = 16 * (o.dma_no // DMA_RING)
                    if seen.get(s.name, 0) < v and waits.get(s.name, (None, 0))[1] < v:
                        waits[s.name] = (s, v)
                for (s, v) in waits.values():
                    eng.wait_ge(s, v)
                    seen[s.name] = v
                inst = o.fn(eng)
                if o.signal:
                    s, v = target(o)
                    inst.then_inc(s, 16 if o.is_dma else 1)
            if eng_name == SP:
                done = {}
                for (s, v) in final_waits:
                    if done.get(s.name, 0) < v:
                        done[s.name] = v
                for (s, v) in final_waits:
                    if done.get(s.name) == v:
                        eng.wait_ge(s, v)
                        done[s.name] = -1

        with nc.Block() as block:
            @block.sync
            def _(e):
                run(SP, e)

            @block.scalar
            def _(e):
                run(ACT, e)

            @block.vector
            def _(e):
                run(DVE, e)

            @block.gpsimd
            def _(e):
                run(POOL, e)

            @block.tensor
            def _(e):
                run(PE, e)
        es.close()
        return {e: len(per_eng[e]) for e in ENGINES}


INPUT_SPECS = [
    ("x", [1024, 1024], F32), ("cvec", [1, 1024], F32),
    ("mod_w", [2, 1024, 6144], F32), ("mod_b", [2, 6144], F32),
    ("norm_mix", [2, 1024], F32), ("norm_ffn", [2, 1024], F32), ("norm_final", [1, 1024], F32),
    ("ev_w_in", [1024, 2816], F32), ("ev_w_out", [1024, 1024], F32),
    ("a_q_norm", [1, 64], F32), ("a_k_norm", [1, 64], F32), ("decay", [1, 16], F32),
    ("od_w_in", [1024, 2304], F32), ("od_w_out", [1024, 1024], F32),
    ("lam4", [4, 64], F32), ("subln", [1, 128], F32), ("sink", [1, 8], F32),
    ("w_q", [2, 1024, 2048], F32), ("subkeys", [2, 16, 128, 128], F32),
    ("u0", [16384, 1024], F32), ("u1", [16384, 1024], F32), ("v0", [16384, 1024], F32), ("v1", [16384, 1024], F32),
    ("cak", [2, 256, 64], F32), ("cav", [2, 256, 64], F32), ("s0f", [8, 64, 64], F32), ("s0b", [8, 64, 64], F32),
    ("cck", [8, 256, 64], F32), ("ccv", [4, 256, 128], F32), ("cdk", [2, 256, 64], F32), ("cdv", [2, 256, 64], F32),
    ("rope_cos", [1024, 32], F32), ("rope_sin", [1024, 32], F32),
    ("mvA", [10, 1024], BF16), ("mvD", [10, 1024], BF16), ("triA", [128, 128], BF16), ("triB", [128, 128], BF16),
    ("sel", [10, 10, 128], BF16), ("selK", [10, 1280], BF16), ("identb", [128, 128], BF16), ("identf", [128, 128], F32),
    ("flags", [1, 16], F32), ("diff", [128, 128], F32), ("keepf", [128, 128], F32), ("keepb", [128, 128], F32),
    ("cols", [128, 4], F32), ("iota128", [128, 128], I32),
]
OUTPUT_SPECS = [
    ("y", [1024, 1024]), ("nak", [2, 1024, 64]), ("nav", [2, 1024, 64]),
    ("nsf", [4, 8, 64, 64]), ("nsb", [4, 8, 64, 64]),
    ("nck", [8, 1024, 64]), ("ncv", [4, 1024, 128]), ("ndk", [2, 1024, 64]), ("ndv", [2, 1024, 64]),
]
DEBUG_SPECS = [("dbg0", [1024, 1024]), ("dbg1", [1024, 1024]), ("dbg2", [1024, 1024]), ("dbg3", [1024, 1024]), ("dbgm", [1024, 1024])]


def build_program():
    nc = bass.Bass("TRN2", target_bir_lowering=False)
    D = {}
    for (n, shp, dt) in INPUT_SPECS:
        if SMALL_UV and n in ("u0", "u1", "v0", "v1"):
            shp = [128, 1024]
        D[n] = nc.dram_tensor(n, shp, dt, kind="ExternalInput").ap()
    for (n, shp) in OUTPUT_SPECS + (DEBUG_SPECS if DEBUG else []):
        D[n] = nc.dram_tensor(n, shp, F32, kind="ExternalOutput").ap()
    es = contextlib.ExitStack()
    P = Prog(nc)

    def sb(name, shape, dt=F32):
        return es.enter_context(nc.sbuf_tensor("s_" + name, shape, dt))

    def psum(name, shape, dt=F32):
        return es.enter_context(nc.psum_tensor("p_" + name, shape, dt))

    def V(fn, r=(), w=()):
        return P.op(DVE, fn, r, w)

    def A(fn, r=(), w=()):
        return P.op(ACT, fn, r, w)

    def T(fn, r=(), w=()):
        return P.op(PE, fn, r, w)

    def G(fn, r=(), w=()):
        return P.op(POOL, fn, r, w)

    xres = sb("xres", [128, 8, 1024])
    modbc = sb("modbc", [128, 6144])
    hT = sb("hT", [128, 8, 1024], BF16)
    identb = sb("identb", [128, 128], BF16)
    identf = sb("identf", [128, 128])
    cos_t = sb("cos_t", [128, 8, 32])
    sin_t = sb("sin_t", [128, 8, 32])
    sel = sb("sel", [10, 10, 128], BF16)
    mvA = sb("mvA", [10, 1024], BF16)
    mvD = sb("mvD", [10, 1024], BF16)
    triA = sb("triA", [128, 128], BF16)
    triB = sb("triB", [128, 128], BF16)
    cols = sb("cols", [128, 4])
    iota = sb("iota", [128, 128], I32)
    small = sb("small", [128, 512])
    R_Q = sb("R_Q", [128, 8192], BF16)
    R_K = sb("R_K", [128, 10240], BF16)
    R_M = sb("R_M", [128, 4096], BF16)
    R_X = sb("R_X", [128, 8192], BF16)
    R_V = sb("R_V", [128, 5376], BF16)
    R_G = sb("R_G", [128, 4096], BF16)
    R_W = sb("R_W", [128, 8192], BF16)
    R_T = sb("R_T", [128, 4096])
    R_P = sb("R_P", [128, 2048], BF16)
    R_S = sb("R_S", [128, 1024])
    if DEBUG:
        print("sbuf bytes remaining/partition:", nc.sbuf_bytes_remaining)

    psA = psum("psA", [128, 2048])
    psB0 = psum("psB0", [128, 512])
    psB1 = psum("psB1", [128, 512])
    psT0 = psum("psT0", [128, 1024], BF16)
    psT1 = psum("psT1", [128, 1024], BF16)
    psS = [psA[:, 0:512], psA[:, 512:1024]]
    psO = psA[:, 1024:2048]
    psB = [psB0[:, :], psB1[:, :]]
    accO = [psA[:, 1024:1536], psA[:, 1536:2048], psB[0], psB[1]]
    ACCK = ["psO", "psB0", "psB1"]
    pO4 = psA[:, 1536:2048]

    tmpf = [R_T[:, i * 1024:(i + 1) * 1024] for i in range(4)]
    wblk = [R_W[:, i * 4096:(i + 1) * 4096].rearrange("p (j n) -> p j n", j=8) for i in range(2)]
    wfull = R_W[:, :].rearrange("p (j n) -> p j n", j=8)
    mixv = R_X[:, :].rearrange("p (t f) -> p t f", t=8)
    QT = R_Q[0:64, :].rearrange("p (h t) -> p h t", h=8)
    QTa = R_Q[0:74, :].rearrange("p (h t) -> p h t", h=8)
    KTa = R_K[0:74, :].rearrange("p (h t) -> p h t", h=8)
    KT = R_K[0:64, :].rearrange("p (h t) -> p h t", h=8)
    PTb = [R_P[:, i * 512:(i + 1) * 512] for i in range(2)]
    hbuf = [R_P[:, 1024:2048], R_G[:, 0:1024]]
    hbufk = ["hbuf0", "G:hbuf1"]

    _sm = [0]

    def sm(n):
        a = small[:, _sm[0]:_sm[0] + n]
        _sm[0] += n
        assert _sm[0] <= 512
        return a

    ssq = sm(8); std = sm(8); rstd = sm(8)
    hss = sm(8); hstd = sm(8); hrs = sm(8)
    dec = sm(16); lgt = sm(16); nlg = sm(16)
    qdf = sm(8); qdb = sm(8); kdf = sm(8); kdb = sm(8); cdf = sm(8); cdb = sm(8)
    flg = sm(16); esink = sm(8); sinkb = sm(8)
    lamt = sm(8); dent = sm(8); rdent = sm(8)
    musum = sm(8); negmu = sm(8); varsum = sm(8); vstd = sm(8); vrs = sm(8)
    gq = sb("gq", [128, 512]); gk = sb("gk", [128, 128])
    csil = sb("csil", [128, 8])
    csilbc = R_X[:, 0:1024].rearrange("p (j n) -> p j n", j=8)

    uid = [0]

    def U(prefix):
        uid[0] += 1
        return f"{prefix}#{uid[0]}"

    P.dma(SP, identb[:], D["identb"], writes=["identb"])
    P.dma(SP, identf[:], D["identf"], writes=["identf"])
    for t in range(8):
        P.dma(SP, xres[:, t, :], D["x"][t * 128:(t + 1) * 128, :], writes=[f"x{t}"])
    P.dma(SP, cos_t[:], D["rope_cos"].rearrange("(t p) d -> p t d", p=128), writes=["cos"])
    P.dma(SP, sin_t[:], D["rope_sin"].rearrange("(t p) d -> p t d", p=128), writes=["sin"])
    P.dma(SP, sel[:], D["sel"], writes=["sel"])
    P.dma(SP, mvA[:], D["mvA"], writes=["mvA"])
    P.dma(SP, mvD[:], D["mvD"], writes=["mvD"])
    P.dma(SP, triA[:], D["triA"], writes=["triA"])
    P.dma(SP, triB[:], D["triB"], writes=["triB"])
    P.dma(SP, cols[:], D["cols"], writes=["cols"])
    P.dma(SP, iota[:], D["iota128"], writes=["iota"])
    P.dma(SP, flg, D["flags"].partition_broadcast(128), writes=["flg"])
    P.dma(SP, csil[:], D["cvec"].rearrange("o (j d) -> d (o j)", d=128), writes=["csil"], allow_slow_non_contiguous=True)
    A(lambda e: e.activation(out=csil[:], in_=csil[:], func=AF.Silu), ["csil"], ["csil"])

    def dbg_dump(k):
        if not DEBUG:
            return
        for t in range(8):
            P.dma(SP, D[f"dbg{k}"][t * 128:(t + 1) * 128, :], xres[:, t, :], reads=[f"x{t}"], is_output=True)

    def compute_mod(l):
        modw_v = D["mod_w"][l].rearrange("(j d) n -> d j n", d=128)
        mkeys = [f"modbc{b}" for b in range(12)]
        V(lambda e: e.tensor_copy(out=csilbc, in_=csil[:].unsqueeze(2).to_broadcast([128, 8, 128])), ["csil"], ["X:csilbc"])
        P.dma(SP, modbc[:, :], D["mod_b"][l:l + 1, :].partition_broadcast(128), writes=mkeys)
        stage = [R_T[:, 0:2048].bitcast(BF16).rearrange("p (j n) -> p j n", j=8),
                 R_T[:, 2048:4096].bitcast(BF16).rearrange("p (j n) -> p j n", j=8)]
        skeys = [["tmpf0", "tmpf0v", "tmpf1", "tmpf1v"], ["tmpf2", "tmpf3a", "tmpf3b"]]
        for blk in range(12):
            st = stage[blk % 2]
            c0 = blk * 512
            P.dma(POOL, st, modw_v[:, :, c0:c0 + 512], writes=skeys[blk % 2])
            pp = psB[blk % 2]

            def mm(e, st=st, pp=pp):
                for j in range(8):
                    i = e.matmul(pp, lhsT=csilbc[:, j, :], rhs=st[:, j, :], start=(j == 0), stop=(j == 7))
                return i
            T(mm, ["X:csilbc"] + skeys[blk % 2], [f"psB{blk % 2}"])
            V(lambda e, c0=c0, pp=pp: e.tensor_tensor(out=modbc[:, c0:c0 + 512], in0=pp, in1=modbc[:, c0:c0 + 512], op=ALU.add),
              [f"psB{blk % 2}", f"modbc{blk}"], [f"modbc{blk}"])
        for (slot, nm) in ((1, "norm_mix"), (4, "norm_ffn")):
            P.dma(SP, tmpf[0], D[nm][l:l + 1, :].partition_broadcast(128), writes=["tmpf0", "tmpf0v"])
            sl = modbc[:, slot * 1024:(slot + 1) * 1024]
            ks = [f"modbc{2 * slot}", f"modbc{2 * slot + 1}"]
            V(lambda e, sl=sl: e.scalar_tensor_tensor(out=sl, in0=sl, scalar=1.0, in1=tmpf[0], op0=ALU.add, op1=ALU.mult),
              ks + ["tmpf0", "tmpf0v"], ks)

    def modv(i):
        return modbc[:, i * 1024:(i + 1) * 1024], [f"modbc{2 * i}", f"modbc{2 * i + 1}"]

    def ada_tile(t, gslot, sslot, out_bf, out_bf_key, out_f32=None, out_f32_key=None):
        Gv, Gk = modv(gslot)
        Sv, Sk = modv(sslot)
        junk = tmpf[3][:, 0:512].bitcast(BF16)
        A(lambda e: e.activation(out=junk, in_=xres[:, t, :], func=AF.Square, accum_out=ssq[:, t:t + 1]), [f"x{t}"], ["tmpf3a", "tmpf3b", f"ssq{t}"])
        A(lambda e: e.activation(out=std[:, t:t + 1], in_=ssq[:, t:t + 1], func=AF.Sqrt, scale=1.0 / 1024, bias=EPS), [f"ssq{t}"], [f"std{t}"])
        V(lambda e: e.reciprocal(out=rstd[:, t:t + 1], in_=std[:, t:t + 1]), [f"std{t}"], [f"rstd{t}"])
        tf = tmpf[t % 2]
        tk = f"tmpf{t % 2}"
        V(lambda e: e.scalar_tensor_tensor(out=tf, in0=xres[:, t, :], scalar=rstd[:, t:t + 1], in1=Gv, op0=ALU.mult, op1=ALU.mult),
          [f"x{t}", f"rstd{t}"] + Gk, [tk, tk + "v"])
        if out_f32 is not None:
            V(lambda e: e.tensor_tensor(out=out_f32, in0=tf, in1=Sv, op=ALU.add), [tk] + Sk, [out_f32_key])
            A(lambda e: e.copy(out=out_bf, in_=out_f32), [out_f32_key], [out_bf_key])
        else:
            V(lambda e: e.tensor_tensor(out=out_bf, in0=tf, in1=Sv, op=ALU.add), [tk] + Sk, [out_bf_key])

    def transpose_tile_to_hT(src_bf, src_key, t, pt, ptk, dst=None, dkey=None):
        def tr(e):
            for j in range(8):
                i = e.transpose(out=pt[:, j * 128:(j + 1) * 128], in_=src_bf[:, j * 128:(j + 1) * 128], identity=identb[:])
            return i
        T(tr, [src_key, "identb"], [ptk])
        d = hT if dst is None else dst
        A(lambda e: e.copy(out=d[:, :, t * 128:(t + 1) * 128], in_=pt[:, :].rearrange("p (j n) -> p j n", j=8)),
          [ptk], [(dkey or "hT") + f"_{t}"])

    def mixer_norm_phase(l):
        for t in range(8):
            hb = hbuf[t % 2]
            hk = hbufk[t % 2]
            ada_tile(t, 1, 0, hb, hk)
            pt = (psT0, psT1)[t % 2]
            transpose_tile_to_hT(hb, hk, t, pt, f"psT{t % 2}")

    wcount = [0]

    def proj_block(wdram, col0, ncols, evac, hkey="hT"):
        i = wcount[0] % 2
        wcount[0] += 1
        wb = wblk[i][:, :, 0:ncols]
        wk = f"W:wblk{i}"
        P.dma(POOL, wb, wdram.rearrange("(j d) n -> d j n", d=128)[:, :, col0:col0 + ncols], writes=[wk])
        for t in range(8):
            pp = psB[t % 2]
            pk = f"psB{t % 2}"

            def mm(e, t=t, pp=pp):
                for j in range(8):
                    ins = e.matmul(pp[:, 0:ncols], lhsT=hT[:, j, t * 128:(t + 1) * 128], rhs=wb[:, j, :], start=(j == 0), stop=(j == 7))
                return ins
            T(mm, [wk, f"{hkey}_{t}"], [pk])
            evac(t, pp, pk)

    def head_transposes(src_bf, src_key, H, dstT, dst_key, h0, t, tok_off=0):
        pt = (psT0, psT1)[t % 2]
        ptk = f"psT{t % 2}"

        def tr(e):
            for h in range(H):
                i = e.transpose(out=pt[0:64, h * 128:(h + 1) * 128], in_=src_bf[:, h, :], identity=identb[:])
            return i
        T(tr, [src_key, "identb"], [ptk])
        c0 = tok_off + t * 128
        A(lambda e: e.copy(out=dstT[:, h0:h0 + H, c0:c0 + 128], in_=pt[0:64, 0:H * 128].rearrange("p (h n) -> p h n", h=H)),
          [ptk], [f"{dst_key}_{t}"])

    def norm_heads(pp, pk, H, gain, gkey, t, dst_f32, dkey):
        sq = tmpf[2][:, 0:H * 64]
        A(lambda e: e.activation(out=sq, in_=pp[:, 0:H * 64], func=AF.Square), [pk], ["tmpf2"])
        V(lambda e: e.tensor_reduce(out=hss[:, 0:H], in_=sq.rearrange("p (h d) -> p h d", d=64), axis=AX.X, op=ALU.add), ["tmpf2"], ["hss"])
        A(lambda e: e.activation(out=hstd[:, 0:H], in_=hss[:, 0:H], func=AF.Sqrt, scale=1.0 / 64, bias=EPS), ["hss"], ["hstd"])
        V(lambda e: e.reciprocal(out=hrs[:, 0:H], in_=hstd[:, 0:H]), ["hstd"], ["hrs"])
        V(lambda e: e.tensor_tensor(out=dst_f32, in0=pp[:, 0:H * 64].rearrange("p (h d) -> p h d", d=64),
                                    in1=hrs[:, 0:H].unsqueeze(2).to_broadcast([128, H, 64]), op=ALU.mult), [pk, "hrs"], [dkey])
        V(lambda e: e.tensor_tensor(out=dst_f32, in0=dst_f32, in1=gain.rearrange("p (h d) -> p h d", d=64), op=ALU.mult), [dkey, gkey], [dkey])

    def rope(src, skey, H, t, dst_bf, dkey, ctab=None, stab=None):
        ctab = cos_t if ctab is None else ctab
        stab = sin_t if stab is None else stab
        cs = ctab[:, t, :].unsqueeze(1).to_broadcast([128, H, 32])
        sn = stab[:, t, :].unsqueeze(1).to_broadcast([128, H, 32])
        x1 = src[:, :, 0:32]
        x2 = src[:, :, 32:64]
        t1 = tmpf[3][:, 0:H * 32].rearrange("p (h d) -> p h d", d=32)
        t2 = tmpf[3][:, 512:512 + H * 32].rearrange("p (h d) -> p h d", d=32)
        V(lambda e: e.tensor_tensor(out=t1, in0=x1, in1=cs, op=ALU.mult), [skey, "cos", "cos8"], ["tmpf3a"])
        V(lambda e: e.tensor_tensor(out=t2, in0=x2, in1=sn, op=ALU.mult), [skey, "sin", "sin8"], ["tmpf3b"])
        V(lambda e: e.tensor_tensor(out=dst_bf[:, :, 0:32], in0=t1, in1=t2, op=ALU.subtract), ["tmpf3a", "tmpf3b"], [dkey])
        V(lambda e: e.tensor_tensor(out=t1, in0=x1, in1=sn, op=ALU.mult), [skey, "sin"], ["tmpf3a"])
        V(lambda e: e.tensor_tensor(out=t2, in0=x2, in1=cs, op=ALU.mult), [skey, "cos"], ["tmpf3b"])
        V(lambda e: e.tensor_tensor(out=dst_bf[:, :, 32:64], in0=t1, in1=t2, op=ALU.add), ["tmpf3a", "tmpf3b"], [dkey])

    def bfstage(t, n):
        return PTb[t % 2][:, 0:n], f"PT{t % 2}"

    def load_cache_k(dram, nh, dstT, dkey, dv_=64):
        for kb in range(2):
            st = tmpf[2][:, 0:nh * 64].rearrange("p (h d) -> p h d", d=64)
            P.dma(SP, st, dram[:, kb * 128:(kb + 1) * 128, :].rearrange("h k d -> k h d"), writes=["tmpf2"])
            sbf, sk = bfstage(kb, nh * 64)
            sbf3 = sbf.rearrange("p (h d) -> p h d", d=64)
            A(lambda e, st=st, sbf3=sbf3: e.copy(out=sbf3, in_=st), ["tmpf2"], [sk])
            head_transposes(sbf3, sk, nh, dstT, dkey + "c", 0, kb, tok_off=0)

    def load_cache_v(dram, nh, dv, vaug, vkey):
        for kb in range(2):
            st = tmpf[2][:, 0:nh * dv].rearrange("p (h d) -> p h d", d=dv)
            P.dma(SP, st, dram[:, kb * 128:(kb + 1) * 128, :].rearrange("h k d -> k h d"), writes=["tmpf2"])
            A(lambda e, st=st, kb=kb: e.copy(out=vaug[:, kb, :, 0:dv], in_=st), ["tmpf2"], [f"{vkey}_{kb}"])

    def attention(nq, nkv, kv_of, v_of, qkey, kkey, vaug, vkey, dv, mvname, tri, finish, transposed):
        V(lambda e: e.memset(small[:, 392:394], 0.0), [], ["psO", "psOa", "psOb"])
        P.dma(SP, QTa[64:74, 0:nq, :], D[mvname].unsqueeze(1).to_broadcast([10, nq, 1024]), writes=["Q:mrows"])
        P.dma(SP, KTa[64:74, 0:nkv, :], D["selK"].unsqueeze(1).to_broadcast([10, nkv, 1280]), writes=["K:srows"])
        its = [(h, qh, kb) for h in range(nq) for qh in range(2) for kb in range(10)]

        def emit_scores(i):
            h, qh, kb = its[i]
            g = kv_of(h)
            pS = psS[i % 2]
            psk = f"psS{i % 2}"
            kkeys = [f"{kkey}c_{kb}"] if kb < 2 else [f"{kkey}_{kb - 2}"]
            qkeys = [f"{qkey}_{qh * 4 + j}" for j in range(4)]
            tris = []
            if tri and kb >= 2:
                for qb in range(4):
                    n = qh * 4 + qb
                    if kb - 2 == n - 1:
                        tris.append((qb, triA))
                    elif kb - 2 == n + 1:
                        tris.append((qb, triB))

            def mm(e):
                ins = e.matmul(pS, lhsT=KTa[:, g, kb * 128:(kb + 1) * 128], rhs=QTa[:, h, qh * 512:(qh + 1) * 512], start=True, stop=(len(tris) == 0))
                for n_, (qb, tr_) in enumerate(tris):
                    ins = e.matmul(pS[:, qb * 128:(qb + 1) * 128], lhsT=identb[:], rhs=tr_[:], start=False, stop=(n_ == len(tris) - 1))
                return ins
            T(mm, ["Q:mrows", "K:srows", "identb", "triA", "triB"] + kkeys + qkeys, [psk])
            pt = PTb[i % 2]
            A(lambda e: e.activation(out=pt, in_=pS, func=AF.Exp), [psk], [f"PT{i % 2}"])

        if transposed:
            accT = psA[0:dv + 1, 1024:1536]
            oT = tmpf[3][0:dv + 1, 0:512]
            pTr4 = psA[:, 1536:2048]

        def emit_pv(i):
            h, qh, kb = its[i]
            gv = v_of(h)
            pt = PTb[i % 2]
            if transposed:
                T(lambda e: e.matmul(accT, lhsT=vaug[:, kb, gv, 0:dv + 1], rhs=pt, start=(kb == 0), stop=(kb == 9)),
                  [f"PT{i % 2}", f"{vkey}_{kb}"], ["psOa"])
                if kb == 9:
                    A(lambda e: e.copy(out=oT, in_=accT), ["psOa"], ["tmpf3a", "tmpf3b"])

                    def tr(e):
                        for qb in range(4):
                            ins = e.transpose(out=pTr4[:, qb * (dv + 1):(qb + 1) * (dv + 1)], in_=oT[:, qb * 128:(qb + 1) * 128], identity=identf[0:dv + 1, 0:dv + 1])
                        return ins
                    T(tr, ["tmpf3a", "tmpf3b", "identf"], ["psOb"])
                return

            def pv(e):
                for qb in range(4):
                    ins = e.matmul(accO[qb][:, 0:dv + 1], lhsT=pt[:, qb * 128:(qb + 1) * 128],
                                   rhs=vaug[:, kb, gv, 0:dv + 1], start=(kb == 0), stop=(kb == 9))
                return ins
            T(pv, [f"PT{i % 2}", f"{vkey}_{kb}"], ACCK)

        emit_scores(0)
        for i in range(len(its)):
            if i + 1 < len(its):
                emit_scores(i + 1)
            emit_pv(i)
            if its[i][2] == 9:
                finish(its[i][0], its[i][1])
        V(lambda e: e.memset(small[:, 392:394], 0.0), [], ["psO", "psOa", "psOb"])

    def layer0():
        compute_mod(0)
        mixer_norm_phase(0)
        P.dma(SP, gk[:, 0:64], D["a_q_norm"].partition_broadcast(128), writes=["gk"])
        V(lambda e: e.tensor_scalar(out=gq[:].rearrange("p (h d) -> p h d", d=64), in0=gk[:, 0:64].unsqueeze(1).to_broadcast([128, 8, 64]),
                                    scalar1=0.125, scalar2=None, op0=ALU.mult), ["gk"], ["gq"])
        gk2 = sb("gk2", [128, 128])
        P.dma(SP, gk2[:, 0:64], D["a_k_norm"].partition_broadcast(128), writes=["gk2"])
        V(lambda e: e.tensor_copy(out=gk2[:, 64:128], in_=gk2[:, 0:64]), ["gk2"], ["gk2"])

        vaugA = R_V[:, 0:10 * 2 * 65].rearrange("p (k g d) -> p k g d", k=10, g=2)
        V(lambda e: e.memset(vaugA[:, :, :, 64:65], 1.0), [], [f"V:va_{kb}" for kb in range(10)])
        load_cache_k(D["cak"], 2, KT, "K:ka")
        load_cache_v(D["cav"], 2, 64, vaugA, "V:va")

        def evac_q(t, pp, pk):
            qn = tmpf[t % 2][:, 0:512].rearrange("p (h d) -> p h d", d=64)
            qk = f"tmpf{t % 2}"
            norm_heads(pp, pk, 8, gq[:], "gq", t, qn, qk)
            sbf, sk = bfstage(t, 512)
            sbf3 = sbf.rearrange("p (h d) -> p h d", d=64)
            rope(qn, qk, 8, t, sbf3, sk)
            head_transposes(sbf3, sk, 8, QT, "Q:qa", 0, t)
        proj_block(D["ev_w_in"], 0, 512, evac_q)

        def evac_kv(t, pp, pk):
            kn = tmpf[t % 2][:, 0:128].rearrange("p (h d) -> p h d", d=64)
            kk = f"tmpf{t % 2}"
            norm_heads(pp, pk, 2, gk2[:], "gk2", t, kn, kk)
            P.dma(SP, D["nak"][:, t * 128:(t + 1) * 128, :].rearrange("h k d -> k h d"), kn, reads=[kk], is_output=True)
            sbf, sk = bfstage(t, 128)
            sbf3 = sbf.rearrange("p (h d) -> p h d", d=64)
            rope(kn, kk, 2, t, sbf3, sk)
            head_transposes(sbf3, sk, 2, KT, "K:ka", 0, t, tok_off=256)
            vf = tmpf[t % 2][:, 512:640].rearrange("p (h d) -> p h d", d=64)
            vfk = f"tmpf{t % 2}v"
            A(lambda e: e.copy(out=vf, in_=pp[:, 128:256].rearrange("p (h d) -> p h d", d=64)), [pk], [vfk])
            P.dma(SP, D["nav"][:, t * 128:(t + 1) * 128, :].rearrange("h k d -> k h d"), vf, reads=[vfk], is_output=True)
            V(lambda e: e.tensor_copy(out=vaugA[:, 2 + t, :, 0:64], in_=vf), [vfk], [f"V:va_{2 + t}"])
        proj_block(D["ev_w_in"], 512, 256, evac_kv)

        def finA(h, qh):
            o4 = pO4[:, 0:260].rearrange("p (q d) -> p q d", q=4)
            V(lambda e: e.reciprocal(out=rdent[:, 0:4], in_=o4[:, :, 64]), ["psOb"], ["rdent"])
            V(lambda e: e.tensor_tensor(out=mixv[:, qh * 4:(qh + 1) * 4, h * 64:(h + 1) * 64], in0=o4[:, :, 0:64],
                                        in1=rdent[:, 0:4].unsqueeze(2).to_broadcast([128, 4, 64]), op=ALU.mult),
              ["psOb", "rdent"], [f"X:mix_{qh * 4 + j}" for j in range(4)])
        attention(8, 2, lambda h: h // 4, lambda h: h // 4, "Q:qa", "K:ka", vaugA, "V:va", 64, "mvA", False, finA, True)
        P.retire("Q", "K", "V", "M", "G", "W")

        P.dma(SP, dec, D["decay"].partition_broadcast(128), writes=["dec"])
        A(lambda e: e.activation(out=lgt, in_=dec, func=AF.Exp, scale=-1.0), ["dec"], ["lgt"])
        A(lambda e: e.activation(out=lgt, in_=lgt, func=AF.Ln, bias=1.0), ["lgt"], ["lgt"])
        V(lambda e: e.tensor_scalar(out=nlg, in0=lgt, scalar1=-1.0, scalar2=None, op0=ALU.mult), ["lgt"], ["nlg"])
        A(lambda e: e.activation(out=qdf, in_=nlg[:, 0:8], func=AF.Exp, scale=cols[:, 0:1]), ["nlg", "cols"], ["qdf"])
        A(lambda e: e.activation(out=qdb, in_=nlg[:, 8:16], func=AF.Exp, scale=cols[:, 1:2]), ["nlg", "cols"], ["qdb"])
        A(lambda e: e.activation(out=kdf, in_=nlg[:, 0:8], func=AF.Exp, scale=cols[:, 2:3]), ["nlg", "cols"], ["kdf"])
        A(lambda e: e.activation(out=kdb, in_=nlg[:, 8:16], func=AF.Exp, scale=cols[:, 3:4]), ["nlg", "cols"], ["kdb"])
        A(lambda e: e.activation(out=cdf, in_=nlg[:, 0:8], func=AF.Exp, scale=128.0), ["nlg"], ["cdf"])
        A(lambda e: e.activation(out=cdb, in_=nlg[:, 8:16], func=AF.Exp, scale=128.0), ["nlg"], ["cdb"])
        Dtot = R_S[:, 0:1024].rearrange("p (h n) -> p h n", h=8)
        dif = tmpf[2][:, 0:128]
        kpf = tmpf[2][:, 128:256]
        kpb = tmpf[2][:, 256:384]
        P.dma(SP, dif, D["diff"], writes=["tmpf2"])
        P.dma(SP, kpf, D["keepf"], writes=["tmpf2"])
        P.dma(SP, kpb, D["keepb"], writes=["tmpf2"])
        for h in range(8):
            e1 = tmpf[3][:, 0:128]
            e2 = tmpf[3][:, 512:640]
            A(lambda e, h=h: e.activation(out=e1, in_=dif, func=AF.Exp, scale=nlg[:, h:h + 1]), ["tmpf2", "nlg"], ["tmpf3a"])
            A(lambda e, h=h: e.activation(out=e2, in_=dif, func=AF.Exp, scale=lgt[:, 8 + h:9 + h]), ["tmpf2", "lgt"], ["tmpf3b"])
            V(lambda e: e.tensor_tensor(out=e1, in0=e1, in1=kpf, op=ALU.mult), ["tmpf3a", "tmpf2"], ["tmpf3a"])
            V(lambda e: e.tensor_tensor(out=e2, in0=e2, in1=kpb, op=ALU.mult), ["tmpf3b", "tmpf2"], ["tmpf3b"])
            V(lambda e, h=h: e.tensor_tensor(out=Dtot[:, h, :], in0=e1, in1=e2, op=ALU.add), ["tmpf3a", "tmpf3b"], ["S:Dtot"])

        def evac_rq(t, pp, pk):
            sbf, sk = bfstage(t, 512)
            A(lambda e: e.copy(out=sbf, in_=pp[:, 0:512]), [pk], [sk])
            head_transposes(sbf.rearrange("p (h d) -> p h d", d=64), sk, 8, QT, "Q:qr", 0, t)
        proj_block(D["ev_w_in"], 768, 512, evac_rq)
        Ktm = R_M[:, :].rearrange("p (c h d) -> p c h d", c=8, h=8)

        def evac_rk(t, pp, pk):
            sk = f"M:Ktm_{t}"
            A(lambda e: e.activation(out=Ktm[:, t, :, :], in_=pp[:, 0:512].rearrange("p (h d) -> p h d", d=64), func=AF.Copy, scale=0.125), [pk], [sk])
            head_transposes(Ktm[:, t, :, :], sk, 8, KT, "K:kr", 0, t)
        proj_block(D["ev_w_in"], 1280, 512, evac_rk)
        Vr = R_V[:, 0:4096].rearrange("p (c n) -> p c n", c=8)

        def evac_rv(t, pp, pk):
            A(lambda e: e.copy(out=Vr[:, t, :], in_=pp[:, 0:512]), [pk], [f"V:Vr_{t}"])
        proj_block(D["ev_w_in"], 1792, 512, evac_rv)
        gsil = R_G[:, :].rearrange("p (c n) -> p c n", c=8)

        def evac_rg(t, pp, pk):
            A(lambda e: e.activation(out=gsil[:, t, :], in_=pp[:, 0:512], func=AF.Silu), [pk], [f"G:gsil_{t}"])
        proj_block(D["ev_w_in"], 2304, 512, evac_rg)

        Ebf = R_W[0:64, 0:4096].rearrange("p (c h d) -> p c h d", c=8, h=8)
        Ebb = R_W[0:64, 4096:8192].rearrange("p (c h d) -> p c h d", c=8, h=8)
        wkeys = []
        P.retire("W")
        Ef = [tmpf[i][0:64, 0:512].rearrange("p (h d) -> p h d", h=8) for i in range(4)]
        Efk = [["tmpf0"], ["tmpf1"], ["tmpf2"], ["tmpf3a", "tmpf3b"]]
        pKV = psB[0][0:64, :].rearrange("p (h d) -> p h d", h=8)
        for (dirn, kdv, kdname, cd, s0name, Eb, ebname, outname, order, flo) in (
                ("f", kdf, "kdf", cdf, "s0f", Ebf, "W:Ebf", "nsf", list(range(8)), 0),
                ("b", kdb, "kdb", cdb, "s0b", Ebb, "W:Ebb", "nsb", list(range(7, -1, -1)), 8)):
            Ecur = Ef[0]
            P.dma(SP, Ecur, D[s0name].rearrange("h d v -> d h v"), writes=Efk[0])
            ecur_k = Efk[0][0]
            c0 = order[0]
            A(lambda e, Eb=Eb, c0=c0, Ecur=Ecur: e.copy(out=Eb[:, c0, :, :], in_=Ecur), [ecur_k], wkeys + [f"{ebname}_{c0}"])
            for n_, c in enumerate(order):
                Kd = PTb[n_ % 2].rearrange("p (h d) -> p h d", d=64)
                kdk = f"PT{n_ % 2}"
                V(lambda e, Kd=Kd, c=c, kdv=kdv: e.tensor_tensor(out=Kd, in0=Ktm[:, c, :, :], in1=kdv.unsqueeze(2).to_broadcast([128, 8, 64]), op=ALU.mult),
                  [f"M:Ktm_{c}", kdname], [kdk])

                def mm(e, Kd=Kd, c=c):
                    for h in range(8):
                        ins = e.matmul(pKV[:, h, :], lhsT=Kd[:, h, :], rhs=Vr[:, c, h * 64:(h + 1) * 64], start=True, stop=True)
                    return ins
                T(mm, [kdk, f"V:Vr_{c}"], ["psB0"])
                tmpE = Ef[1]
                V(lambda e, Ecur=Ecur, cd=cd: e.tensor_tensor(out=tmpE, in0=Ecur, in1=cd[0:64, :].unsqueeze(2).to_broadcast([64, 8, 64]), op=ALU.mult),
                  [ecur_k, "cd" + dirn], Efk[1])
                aft = Ef[2 + (n_ % 2)]
                aft_ks = Efk[2 + (n_ % 2)]
                aft_k = aft_ks[0]
                V(lambda e, aft=aft: e.tensor_tensor(out=aft, in0=tmpE, in1=pKV, op=ALU.add), Efk[1] + ["psB0"], aft_ks)
                is_out = (c % 2 == 1) if dirn == "f" else (c % 2 == 0)
                if is_out:
                    P.dma(SP, D[outname][c // 2].rearrange("h d v -> d h v"), aft, reads=[aft_k], is_output=True)
                if n_ < 7:
                    cn = order[n_ + 1]
                    V(lambda e, aft=aft, cn=cn, flo=flo: e.tensor_scalar(out=Ef[0], in0=aft, scalar1=flg[0:64, flo + cn:flo + cn + 1], scalar2=None, op0=ALU.mult),
                      [aft_k, "flg"], Efk[0])
                    Ecur = Ef[0]
                    ecur_k = Efk[0][0]
                    A(lambda e, Eb=Eb, cn=cn: e.copy(out=Eb[:, cn, :, :], in_=Ef[0]), Efk[0], [f"{ebname}_{cn}"])

        STD = [PTb[0].rearrange("p (h n) -> p h n", h=4), PTb[1].rearrange("p (h n) -> p h n", h=4)]
        pOI = psO[:, 0:512]
        pQf = psO[:, 512:1024]
        pQb = psB[1]
        for c in range(8):
            cs = slice(c * 128, (c + 1) * 128)
            for grp in range(2):
                pS = psS[grp]

                def mm(e, grp=grp, pS=pS, cs=cs):
                    for hh in range(4):
                        h = grp * 4 + hh
                        ins = e.matmul(pS[:, hh * 128:(hh + 1) * 128], lhsT=KT[:, h, cs], rhs=QT[:, h, cs], start=True, stop=True)
                    return ins
                T(mm, [f"K:kr_{c}", f"Q:qr_{c}"], [f"psS{grp}"])
                V(lambda e, grp=grp, pS=pS: e.tensor_tensor(out=STD[grp], in0=pS.rearrange("p (h n) -> p h n", h=4),
                                                            in1=Dtot[:, grp * 4:(grp + 1) * 4, :], op=ALU.mult), [f"psS{grp}", "S:Dtot"], [f"PT{grp}"])

            def mm2(e, c=c, cs=cs):
                for h in range(8):
                    e.matmul(pOI[:, h * 64:(h + 1) * 64], lhsT=STD[h // 4][:, h % 4, :], rhs=Vr[:, c, h * 64:(h + 1) * 64], start=True, stop=True)
                for h in range(8):
                    e.matmul(pQf[:, h * 64:(h + 1) * 64], lhsT=QT[:, h, cs], rhs=Ebf[:, c, h, :], start=True, stop=True)
                for h in range(8):
                    ins = e.matmul(pQb[:, h * 64:(h + 1) * 64], lhsT=QT[:, h, cs], rhs=Ebb[:, c, h, :], start=True, stop=True)
                return ins
            T(mm2, ["PT0", "PT1", f"V:Vr_{c}", f"Q:qr_{c}", f"W:Ebf_{c}", f"W:Ebb_{c}"], ["psO", "psB1"])
            o = tmpf[0][:, 0:512]
            t1 = tmpf[1][:, 0:512]
            t2 = tmpf[2][:, 0:512]
            o3 = o.rearrange("p (h d) -> p h d", d=64)
            V(lambda e: e.tensor_tensor(out=t1.rearrange("p (h d) -> p h d", d=64), in0=pQf.rearrange("p (h d) -> p h d", d=64),
                                        in1=qdf.unsqueeze(2).to_broadcast([128, 8, 64]), op=ALU.mult), ["psO", "qdf"], ["tmpf1"])
            V(lambda e: e.tensor_tensor(out=t2.rearrange("p (h d) -> p h d", d=64), in0=pQb.rearrange("p (h d) -> p h d", d=64),
                                        in1=qdb.unsqueeze(2).to_broadcast([128, 8, 64]), op=ALU.mult), ["psB1", "qdb"], ["tmpf2"])
            V(lambda e: e.tensor_tensor(out=o, in0=pOI, in1=t1, op=ALU.add), ["psO", "tmpf1"], ["tmpf0"])
            V(lambda e: e.tensor_tensor(out=o, in0=o, in1=t2, op=ALU.add), ["tmpf0", "tmpf2"], ["tmpf0"])
            V(lambda e: e.tensor_reduce(out=musum, in_=o3, axis=AX.X, op=ALU.add), ["tmpf0"], ["musum"])
            V(lambda e: e.tensor_scalar(out=negmu, in0=musum, scalar1=-1.0 / 64, scalar2=None, op0=ALU.mult), ["musum"], ["negmu"])
            V(lambda e: e.tensor_tensor(out=o3, in0=o3, in1=negmu.unsqueeze(2).to_broadcast([128, 8, 64]), op=ALU.add), ["tmpf0", "negmu"], ["tmpf0"])
            A(lambda e: e.activation(out=t1, in_=o, func=AF.Square), ["tmpf0"], ["tmpf1"])
            V(lambda e: e.tensor_reduce(out=varsum, in_=t1.rearrange("p (h d) -> p h d", d=64), axis=AX.X, op=ALU.add), ["tmpf1"], ["varsum"])
            A(lambda e: e.activation(out=vstd, in_=varsum, func=AF.Sqrt, scale=1.0 / 64, bias=EPS), ["varsum"], ["vstd"])
            V(lambda e: e.reciprocal(out=vrs, in_=vstd), ["vstd"], ["vrs"])
            V(lambda e: e.tensor_tensor(out=o3, in0=o3, in1=vrs.unsqueeze(2).to_broadcast([128, 8, 64]), op=ALU.mult), ["tmpf0", "vrs"], ["tmpf0"])
            V(lambda e, c=c: e.tensor_tensor(out=mixv[:, c, 512:1024], in0=o, in1=gsil[:, c, :], op=ALU.mult), ["tmpf0", f"G:gsil_{c}"], [f"X:mix_{c}"])

        P.retire("Q", "K", "V", "M", "G", "W")
        out_proj(D["ev_w_out"])

    def out_proj(wdram):
        for t in range(8):
            pt = (psT0, psT1)[t % 2]
            transpose_tile_to_hT(mixv[:, t, :], f"X:mix_{t}", t, pt, f"psT{t % 2}", dkey="hT")
        wv = wdram.rearrange("(j d) n -> d j n", d=128)
        for half in range(2):
            P.dma(POOL, wfull[:, :, half * 512:(half + 1) * 512], wv[:, :, half * 512:(half + 1) * 512],
                  reads=[f"hT_{t}" for t in range(8)], writes=["W:wblk0", "W:wblk1"])
        g2, g2k = modv(2)
        cnt = 0
        for t in range(8):
            for nb in range(2):
                pp = psB[cnt % 2]
                pk = f"psB{cnt % 2}"
                cnt += 1

                def mm(e, t=t, nb=nb, pp=pp):
                    for j in range(8):
                        ins = e.matmul(pp, lhsT=hT[:, j, t * 128:(t + 1) * 128], rhs=wfull[:, j, nb * 512:(nb + 1) * 512], start=(j == 0), stop=(j == 7))
                    return ins
                T(mm, ["W:wblk0", "W:wblk1", f"hT_{t}"], [pk])
                tt = tmpf[cnt % 2][:, 0:512]
                ttk = f"tmpf{cnt % 2}"
                V(lambda e, pp=pp, tt=tt, nb=nb: e.tensor_tensor(out=tt, in0=pp, in1=g2[:, nb * 512:(nb + 1) * 512], op=ALU.mult), [pk] + g2k, [ttk])
                V(lambda e, t=t, tt=tt, nb=nb: e.tensor_tensor(out=xres[:, t, nb * 512:(nb + 1) * 512], in0=xres[:, t, nb * 512:(nb + 1) * 512], in1=tt, op=ALU.add),
                  [ttk, f"x{t}"], [f"x{t}"])


    def layer1():
        lam_init = 0.8 - 0.6 * math.exp(-0.3 * 1)
        compute_mod(1)
        mixer_norm_phase(1)
        l4 = tmpf[2][:, 0:256].rearrange("p (a d) -> p a d", a=4)
        P.dma(SP, tmpf[2][:, 0:256], D["lam4"].rearrange("(o a) d -> o (a d)", o=1).partition_broadcast(128), writes=["tmpf2"])
        pr = tmpf[2][:, 256:384].rearrange("p (a d) -> p a d", a=2)
        V(lambda e: e.tensor_tensor(out=pr[:, 0, :], in0=l4[:, 0, :], in1=l4[:, 1, :], op=ALU.mult), ["tmpf2"], ["tmpf2"])
        V(lambda e: e.tensor_tensor(out=pr[:, 1, :], in0=l4[:, 2, :], in1=l4[:, 3, :], op=ALU.mult), ["tmpf2"], ["tmpf2"])
        V(lambda e: e.tensor_reduce(out=lamt[:, 0:2], in_=pr, axis=AX.X, op=ALU.add), ["tmpf2"], ["lamt"])
        A(lambda e: e.activation(out=lamt[:, 2:4], in_=lamt[:, 0:2], func=AF.Exp), ["lamt"], ["lamt"])
        V(lambda e: e.tensor_tensor(out=lamt[:, 4:5], in0=lamt[:, 3:4], in1=lamt[:, 2:3], op=ALU.subtract), ["lamt"], ["lamt"])
        V(lambda e: e.tensor_scalar(out=lamt[:, 5:6], in0=lamt[:, 4:5], scalar1=-lam_init, scalar2=None, op0=ALU.add), ["lamt"], ["neglam"])
        neglam = lamt[:, 5:6]
        sgain = gk[:, 0:128]
        P.dma(SP, sgain, D["subln"].partition_broadcast(128), writes=["gk"])
        V(lambda e: e.tensor_scalar(out=sgain, in0=sgain, scalar1=1.0 - lam_init, scalar2=None, op0=ALU.mult), ["gk"], ["gk"])
        P.dma(SP, sinkb, D["sink"].partition_broadcast(128), writes=["sinkb"])
        A(lambda e: e.activation(out=esink, in_=sinkb, func=AF.Exp), ["sinkb"], ["esink"])

        vaugC = R_V[:, 0:10 * 4 * 129].rearrange("p (k g d) -> p k g d", k=10, g=4)
        V(lambda e: e.memset(vaugC[:, :, :, 128:129], 1.0), [], [f"V:vc_{kb}" for kb in range(10)])
        load_cache_k(D["cck"], 8, KT, "K:kc")
        load_cache_v(D["ccv"], 4, 128, vaugC, "V:vc")

        def evac_cq(t, pp, pk):
            qf = tmpf[t % 2][:, 0:512].rearrange("p (h d) -> p h d", d=64)
            qk = f"tmpf{t % 2}"
            A(lambda e: e.activation(out=qf, in_=pp[:, 0:512].rearrange("p (h d) -> p h d", d=64), func=AF.Copy, scale=0.125), [pk], [qk])
            sbf, sk = bfstage(t, 512)
            sbf3 = sbf.rearrange("p (h d) -> p h d", d=64)
            rope(qf, qk, 8, t, sbf3, sk)
            head_transposes(sbf3, sk, 8, QT, "Q:qc", 0, t)
        proj_block(D["od_w_in"], 0, 512, evac_cq)

        def evac_ck(t, pp, pk):
            kf = tmpf[t % 2][:, 0:512].rearrange("p (h d) -> p h d", d=64)
            kk = f"tmpf{t % 2}"
            A(lambda e: e.copy(out=kf, in_=pp[:, 0:512].rearrange("p (h d) -> p h d", d=64)), [pk], [kk])
            P.dma(SP, D["nck"][:, t * 128:(t + 1) * 128, :].rearrange("h k d -> k h d"), kf, reads=[kk], is_output=True)
            sbf, sk = bfstage(t, 512)
            sbf3 = sbf.rearrange("p (h d) -> p h d", d=64)
            rope(kf, kk, 8, t, sbf3, sk)
            head_transposes(sbf3, sk, 8, KT, "K:kc", 0, t, tok_off=256)
        proj_block(D["od_w_in"], 512, 512, evac_ck)

        def evac_cv(t, pp, pk):
            vf = tmpf[t % 2][:, 0:512].rearrange("p (h d) -> p h d", d=128)
            vk = f"tmpf{t % 2}"
            A(lambda e: e.copy(out=vf, in_=pp[:, 0:512].rearrange("p (h d) -> p h d", d=128)), [pk], [vk])
            P.dma(SP, D["ncv"][:, t * 128:(t + 1) * 128, :].rearrange("h k d -> k h d"), vf, reads=[vk], is_output=True)
            V(lambda e: e.tensor_copy(out=vaugC[:, 2 + t, :, 0:128], in_=vf), [vk], [f"V:vc_{2 + t}"])
        proj_block(D["od_w_in"], 1024, 512, evac_cv)


        def finC(hc, qh):
            h = hc // 2
            o = tmpf[hc % 2][:, qh * 512:(qh + 1) * 512].rearrange("p (q d) -> p q d", q=4)
            ok = f"tmpf{hc % 2}" + ("v" if qh else "")
            for q in range(4):
                V(lambda e, q=q: e.reciprocal(out=rdent[:, q:q + 1], in_=accO[q][:, 128:129]), ACCK, [f"rdent{q}"])
                V(lambda e, q=q: e.tensor_scalar(out=o[:, q, :], in0=accO[q][:, 0:128], scalar1=rdent[:, q:q + 1], scalar2=None, op0=ALU.mult),
                  ACCK + [f"rdent{q}"], [ok])
            if hc % 2 == 1:
                o0 = tmpf[0][:, qh * 512:(qh + 1) * 512]
                o1 = tmpf[1][:, qh * 512:(qh + 1) * 512]
                k0 = "tmpf0" + ("v" if qh else "")
                k1 = "tmpf1" + ("v" if qh else "")
                V(lambda e: e.scalar_tensor_tensor(out=o0, in0=o1, scalar=neglam, in1=o0, op0=ALU.mult, op1=ALU.add), [k0, k1, "neglam"], [k0])
                sq = tmpf[2][:, 0:512]
                A(lambda e: e.activation(out=sq, in_=o0, func=AF.Square), [k0], ["tmpf2"])
                V(lambda e: e.tensor_reduce(out=hss[:, 0:4], in_=sq.rearrange("p (q d) -> p q d", q=4), axis=AX.X, op=ALU.add), ["tmpf2"], ["hss"])
                A(lambda e: e.activation(out=hstd[:, 0:4], in_=hss[:, 0:4], func=AF.Sqrt, scale=1.0 / 128, bias=EPS), ["hss"], ["hstd"])
                V(lambda e: e.reciprocal(out=hrs[:, 0:4], in_=hstd[:, 0:4]), ["hstd"], ["hrs"])
                o03 = o0.rearrange("p (q d) -> p q d", q=4)
                V(lambda e: e.tensor_tensor(out=o03, in0=o03, in1=hrs[:, 0:4].unsqueeze(2).to_broadcast([128, 4, 128]), op=ALU.mult), [k0, "hrs"], [k0])
                dst = mixv[:, qh * 4:(qh + 1) * 4, h * 128:(h + 1) * 128]
                V(lambda e: e.tensor_tensor(out=dst, in0=o03, in1=sgain.unsqueeze(1).to_broadcast([128, 4, 128]), op=ALU.mult),
                  [k0, "gk"], [f"X:mix_{qh * 4 + j}" for j in range(4)])
        attention(8, 8, lambda h: h, lambda h: h // 2, "Q:qc", "K:kc", vaugC, "V:vc", 128, "mvA", False, finC, False)
        P.retire("Q", "K", "V")

        vaugD = R_V[:, 0:10 * 2 * 65].rearrange("p (k g d) -> p k g d", k=10, g=2)
        V(lambda e: e.memset(vaugD[:, :, :, 64:65], 1.0), [], [f"V:vd_{kb}" for kb in range(10)])
        load_cache_k(D["cdk"], 2, KT, "K:kd")
        load_cache_v(D["cdv"], 2, 64, vaugD, "V:vd")

        def evac_dq(t, pp, pk):
            qf = tmpf[t % 2][:, 0:512].rearrange("p (h d) -> p h d", d=64)
            qk = f"tmpf{t % 2}"
            A(lambda e: e.activation(out=qf, in_=pp[:, 0:512].rearrange("p (h d) -> p h d", d=64), func=AF.Copy, scale=0.125), [pk], [qk])
            sbf, sk = bfstage(t, 512)
            sbf3 = sbf.rearrange("p (h d) -> p h d", d=64)
            rope(qf, qk, 8, t, sbf3, sk)
            head_transposes(sbf3, sk, 8, QT, "Q:qd", 0, t)
        proj_block(D["od_w_in"], 1536, 512, evac_dq)

        def evac_dkv(t, pp, pk):
            kvf = tmpf[t % 2][:, 0:256].rearrange("p (h d) -> p h d", d=64)
            kk = f"tmpf{t % 2}"
            A(lambda e: e.copy(out=kvf, in_=pp[:, 0:256].rearrange("p (h d) -> p h d", d=64)), [pk], [kk])
            P.dma(SP, D["ndk"][:, t * 128:(t + 1) * 128, :].rearrange("h k d -> k h d"), kvf[:, 0:2, :], reads=[kk], is_output=True)
            P.dma(SP, D["ndv"][:, t * 128:(t + 1) * 128, :].rearrange("h k d -> k h d"), kvf[:, 2:4, :], reads=[kk], is_output=True)
            sbf, sk = bfstage(t, 128)
            sbf3 = sbf.rearrange("p (h d) -> p h d", d=64)
            rope(kvf[:, 0:2, :], kk, 2, t, sbf3, sk)
            head_transposes(sbf3, sk, 2, KT, "K:kd", 0, t, tok_off=256)
            V(lambda e: e.tensor_copy(out=vaugD[:, 2 + t, :, 0:64], in_=kvf[:, 2:4, :]), [kk], [f"V:vd_{2 + t}"])
        proj_block(D["od_w_in"], 2048, 256, evac_dkv)

        def finD(h, qh):
            o4 = pO4[:, 0:260].rearrange("p (q d) -> p q d", q=4)
            V(lambda e: e.tensor_scalar(out=dent[:, 0:4], in0=o4[:, :, 64], scalar1=esink[:, h:h + 1], scalar2=None, op0=ALU.add), ["psOb", "esink"], ["dent"])
            V(lambda e: e.reciprocal(out=rdent[:, 0:4], in_=dent[:, 0:4]), ["dent"], ["rdent"])
            V(lambda e: e.tensor_tensor(out=mixv[:, qh * 4:(qh + 1) * 4, 512 + h * 64:512 + (h + 1) * 64], in0=o4[:, :, 0:64],
                                        in1=rdent[:, 0:4].unsqueeze(2).to_broadcast([128, 4, 64]), op=ALU.mult),
              ["psOb", "rdent"], [f"X:mix_{qh * 4 + j}" for j in range(4)])
        attention(8, 2, lambda h: h // 4, lambda h: h // 4, "Q:qd", "K:kd", vaugD, "V:vd", 64, "mvD", True, finD, True)
        if DEBUG:
            for t in range(8):
                P.dma(POOL, D["dbgm"][t * 128:(t + 1) * 128, :], mixv[:, t, :], reads=[f"X:mix_{t}"], is_output=True)
        P.retire("Q", "K", "V", "M", "G", "W")
        out_proj(D["od_w_out"])

    SHARED = ("Q", "K", "V", "M", "G", "W", "X", "S")

    def peer(l):
        P.retire(*SHARED)
        U_d = D[f"u{l}"]
        V_d = D[f"v{l}"]
        wqv = D["w_q"][l].rearrange("(j d) n -> d j n", d=128)
        wqA = R_Q[:, :].rearrange("p (j n) -> p j n", j=8)
        wqB = R_X[:, :].rearrange("p (j n) -> p j n", j=8)
        for half in range(2):
            P.dma(POOL, wqA[:, :, half * 512:(half + 1) * 512], wqv[:, :, half * 512:(half + 1) * 512], writes=["Q:wq"])
        for half in range(2):
            P.dma(POOL, wqB[:, :, half * 512:(half + 1) * 512], wqv[:, :, 1024 + half * 512:1024 + (half + 1) * 512], writes=["X:wq"])
        skf = R_T[:, 0:2048].rearrange("p (k d) -> p k d", k=16)
        P.dma(SP, skf, D["subkeys"][l].rearrange("k n d -> n k d"), writes=["tmpf0", "tmpf0v", "tmpf1", "tmpf1v"])
        skb = R_V[:, 2048:4096].rearrange("p (k d) -> p k d", k=16)
        A(lambda e: e.copy(out=skb, in_=skf), ["tmpf0", "tmpf1"], ["V:skb"])
        subkT = R_G[:, 2048:4096].rearrange("p (k n) -> p k n", k=16)
        for half in range(2):
            pt = (psT0, psT1)[half]

            def tr(e, half=half, pt=pt):
                for i in range(8):
                    ins = e.transpose(out=pt[:, i * 128:(i + 1) * 128], in_=skb[:, half * 8 + i, :], identity=identb[:])
                return ins
            T(tr, ["V:skb", "identb"], [f"psT{half}"])
            A(lambda e, half=half, pt=pt: e.copy(out=subkT[:, half * 8:(half + 1) * 8, :], in_=pt[:, :].rearrange("p (k n) -> p k n", k=8)),
              [f"psT{half}"], ["G:subkT"])

        idx_all = R_S[:, :].bitcast(I32).rearrange("p (t k) -> p t k", t=8)
        g_all = R_M[:, 0:2048].bitcast(F32).rearrange("p (t k) -> p t k", t=8)
        hf = R_M[:, 2048:4096].bitcast(F32)
        qTt = R_V[:, 0:2048].rearrange("p (k n) -> p k n", k=16)
        keys = R_T[:, 0:2048]
        keys2 = R_T[:, 2048:4096]
        KEYS = ["tmpf0", "tmpf0v", "tmpf1", "tmpf1v"]
        KEYS2 = ["tmpf2", "tmpf3a", "tmpf3b"]
        candr = R_K[:, 0:4096].bitcast(F32)
        cidxr = R_K[:, 4096:8192].bitcast(I32)
        top = R_K[:, 8192:8704].bitcast(F32).rearrange("p (k n) -> p k n", k=16)
        ii = R_K[:, 8704:9216].bitcast(I32).rearrange("p (k n) -> p k n", k=16)
        ctop = R_K[:, 9216:9472].bitcast(F32).rearrange("p (h n) -> p h n", h=8)
        egt = R_K[:, 9472:9728].bitcast(F32).rearrange("p (h n) -> p h n", h=8)
        negm = small[:, 400:408]
        zsum = small[:, 408:416]
        rz = small[:, 416:424]
        bankk = ["psS0", "psS1", "psO", "psO"]

        qTg = R_W[:, :].rearrange("p (k n) -> p k n", k=16)
        psAb = [psA[:, b * 512:(b + 1) * 512] for b in range(4)]
        for t in range(8):
            ada_tile(t, 4, 3, hbuf[t % 2], hbufk[t % 2])
            transpose_tile_to_hT(hbuf[t % 2], hbufk[t % 2], t, (psT0, psT1)[t % 2], f"psT{t % 2}")
        for t in range(8):
            grp, tt = t // 4, t % 4
            if tt == 0:
                for hc in range(16):
                    def mmq(e, hc=hc, grp=grp):
                        w = wqA if hc < 8 else wqB
                        c0 = (hc % 8) * 128
                        for j in range(8):
                            ins = e.matmul(psAb[hc % 4], lhsT=w[:, j, c0:c0 + 128], rhs=hT[:, j, grp * 512:(grp + 1) * 512],
                                           start=(j == 0), stop=(j == 7))
                        return ins
                    T(mmq, ["Q:wq", "X:wq"] + [f"hT_{grp * 4 + j}" for j in range(4)], [bankk[hc % 4]])
                    A(lambda e, hc=hc: e.copy(out=qTg[:, hc, :], in_=psAb[hc % 4]), [bankk[hc % 4]], [f"W:qTg_{hc}"])
            for b in range(4):
                def mms(e, b=b, tt=tt):
                    for hh in range(4):
                        hc = b * 4 + hh
                        ins = e.matmul(psA[:, hc * 128:(hc + 1) * 128], lhsT=qTg[:, hc, tt * 128:(tt + 1) * 128], rhs=subkT[:, hc, :], start=True, stop=True)
                    return ins
                T(mms, [f"W:qTg_{b * 4 + hh}" for hh in range(4)] + ["G:subkT"], [bankk[b]])
                A(lambda e, b=b: e.copy(out=keys[:, b * 512:(b + 1) * 512], in_=psA[:, b * 512:(b + 1) * 512]),
                  [bankk[b]], [KEYS[b]])
            ki = keys.bitcast(I32)
            V(lambda e: e.tensor_scalar(out=ki, in0=ki, scalar1=-128, scalar2=None, op0=ALU.bitwise_and), KEYS, KEYS)
            V(lambda e: e.tensor_tensor(out=ki.rearrange("p (k n) -> p k n", k=16), in0=ki.rearrange("p (k n) -> p k n", k=16),
                                        in1=iota[:, :].unsqueeze(1).to_broadcast([128, 16, 128]), op=ALU.bitwise_or), KEYS + ["iota"], KEYS)
            for hc in range(16):
                kv = keys[:, hc * 128:(hc + 1) * 128]
                V(lambda e, hc=hc, kv=kv: e.max(out=top[:, hc, 0:8], in_=kv), KEYS, [f"K:topa{hc}"])
            for hc in range(16):
                kv = keys[:, hc * 128:(hc + 1) * 128]
                kv2 = keys2[:, hc * 128:(hc + 1) * 128]
                V(lambda e, hc=hc, kv=kv, kv2=kv2: e.match_replace(out=kv2, in_to_replace=top[:, hc, 0:8], in_values=kv, imm_value=-1e30),
                  KEYS + [f"K:topa{hc}"], [f"k2_{hc}"] + (KEYS2 if hc == 0 else []))
            for hc in range(16):
                kv2 = keys2[:, hc * 128:(hc + 1) * 128]
                V(lambda e, hc=hc, kv2=kv2: e.max(out=top[:, hc, 8:16], in_=kv2), [f"k2_{hc}"] + KEYS2, [f"K:topb{hc}"])
            TOPK = [f"K:topa{hc}" for hc in range(16)] + [f"K:topb{hc}" for hc in range(16)]
            topi = top.bitcast(I32)
            V(lambda e: e.tensor_scalar(out=ii, in0=topi, scalar1=127, scalar2=None, op0=ALU.bitwise_and), TOPK, ["K:ii"])
            ii4 = ii.rearrange("p (h c) n -> p h c n", c=2)
            top4 = top.rearrange("p (h c) n -> p h c n", c=2)
            V(lambda e: e.tensor_scalar(out=ii4[:, :, 0, :], in0=ii4[:, :, 0, :], scalar1=7, scalar2=None, op0=ALU.logical_shift_left), ["K:ii"], ["K:ii"])
            cand4 = candr.rearrange("p (h a b) -> p h a b", h=8, a=16)
            cidx4 = cidxr.rearrange("p (h a b) -> p h a b", h=8, a=16)
            V(lambda e: e.tensor_tensor(out=cand4, in0=top4[:, :, 0, :].unsqueeze(3).to_broadcast([128, 8, 16, 16]),
                                        in1=top4[:, :, 1, :].unsqueeze(2).to_broadcast([128, 8, 16, 16]), op=ALU.add), TOPK, ["K:cand"])
            V(lambda e: e.tensor_tensor(out=cidx4, in0=ii4[:, :, 0, :].unsqueeze(3).to_broadcast([128, 8, 16, 16]),
                                        in1=ii4[:, :, 1, :].unsqueeze(2).to_broadcast([128, 8, 16, 16]), op=ALU.bitwise_or), ["K:ii"], ["K:cidx"])
            ci = candr.bitcast(I32)
            V(lambda e: e.tensor_scalar(out=ci, in0=ci, scalar1=-16384, scalar2=None, op0=ALU.bitwise_and), ["K:cand"], ["K:cand"])
            V(lambda e: e.tensor_tensor(out=ci, in0=ci, in1=cidxr, op=ALU.bitwise_or), ["K:cand", "K:cidx"], ["K:cand"])
            for h in range(8):
                cv = candr[:, h * 256:(h + 1) * 256]
                V(lambda e, h=h, cv=cv: e.max(out=ctop[:, h, 0:8], in_=cv), ["K:cand"], [f"K:ctopa{h}"])
            for h in range(8):
                cv = candr[:, h * 256:(h + 1) * 256]
                cv2 = keys[:, h * 256:(h + 1) * 256]
                V(lambda e, h=h, cv=cv, cv2=cv2: e.match_replace(out=cv2, in_to_replace=ctop[:, h, 0:8], in_values=cv, imm_value=-1e30),
                  ["K:cand", f"K:ctopa{h}"], [f"c2_{h}"] + (KEYS if h == 0 else []))
            for h in range(8):
                cv2 = keys[:, h * 256:(h + 1) * 256]
                V(lambda e, h=h, cv2=cv2: e.max(out=ctop[:, h, 8:16], in_=cv2), [f"c2_{h}"] + KEYS, [f"K:ctopb{h}"])
            CTOP = [f"K:ctopa{h}" for h in range(8)] + [f"K:ctopb{h}" for h in range(8)]
            V(lambda e, t=t: e.tensor_scalar(out=idx_all[:, t, :], in0=ctop.bitcast(I32).rearrange("p h n -> p (h n)"), scalar1=16383, scalar2=None, op0=ALU.bitwise_and),
              CTOP, [f"S:idx_{t}"])
            V(lambda e: e.tensor_scalar(out=negm, in0=ctop[:, :, 0], scalar1=-1.0, scalar2=None, op0=ALU.mult), CTOP, ["negm"])
            for h in range(8):
                A(lambda e, h=h: e.activation(out=egt[:, h, :], in_=ctop[:, h, :], func=AF.Exp, bias=negm[:, h:h + 1], accum_out=zsum[:, h:h + 1]),
                  CTOP + ["negm"], [f"K:egt{h}", f"zsum{h}"])
            V(lambda e: e.reciprocal(out=rz, in_=zsum), [f"zsum{h}" for h in range(8)], ["rz"])
            V(lambda e, t=t: e.tensor_tensor(out=g_all[:, t, :].rearrange("p (h n) -> p h n", h=8), in0=egt, in1=rz.unsqueeze(2).to_broadcast([128, 8, 16]), op=ALU.mult),
              [f"K:egt{h}" for h in range(8)] + ["rz"], [f"M:g_{t}"])

        P.retire("Q", "K", "V", "G", "W", "X")
        ring = []
        for (buf, n, nm) in ((R_K, 5, "K"), (R_W, 4, "W"), (R_V, 2, "V"), (R_Q, 4, "Q"), (R_X, 4, "X")):
            for i in range(n):
                ring.append((buf[:, i * 2048:(i + 1) * 2048].bitcast(F32), f"{nm}:ring{i}"))
        NR = len(ring)
        accs = [R_G[:, 0:2048].bitcast(F32), R_G[:, 2048:4096].bitcast(F32)]
        acck = ["G:acc0", "G:acc1"]
        Araw = small[:, 256:384]
        g2, g2k = modv(5)
        rc = [0]
        xx = R_P[:, 0:1024].bitcast(F32)
        x2 = xx[:, 0:128]; inner = xx[:, 128:256]; sg = xx[:, 256:384]; wgt = xx[:, 384:512]
        for t in range(8):
            ada_tile(t, 4, 3, hbuf[0], "hbuf0", out_f32=hf, out_f32_key="M:hf")
            for k in range(128):
                buf, bk = ring[rc[0] % NR]
                rc[0] += 1
                P.op(POOL, lambda e, buf=buf, t=t, k=k: e.indirect_dma_start(out=buf, out_offset=None, in_=U_d,
                                                                              in_offset=bass.IndirectOffsetOnAxis(ap=idx_all[:, t, k:k + 1], axis=0)),
                     [f"S:idx_{t}"], [bk], dma=True)
                V(lambda e, buf=buf, k=k: e.scalar_tensor_tensor(out=buf, in0=buf, scalar=1.0, in1=hf, op0=ALU.mult, op1=ALU.mult, accum_out=Araw[:, k:k + 1]),
                  [bk, "M:hf"], [bk, "Araw"])
            V(lambda e: e.memset(small[:, 384:392], 0.0), [], ["dummy"])
            V(lambda e: e.tensor_tensor(out=x2, in0=Araw, in1=Araw, op=ALU.mult), ["Araw"], ["PT0"])
            V(lambda e: e.tensor_scalar(out=x2, in0=x2, scalar1=0.044715, scalar2=1.0, op0=ALU.mult, op1=ALU.add), ["PT0"], ["PT0"])
            V(lambda e: e.tensor_tensor(out=inner, in0=x2, in1=Araw, op=ALU.mult), ["PT0", "Araw"], ["PT0"])
            A(lambda e: e.activation(out=sg, in_=inner, func=AF.Sigmoid, scale=1.5957691216057308), ["PT0"], ["PT1"])
            V(lambda e: e.tensor_tensor(out=wgt, in0=sg, in1=Araw, op=ALU.mult), ["PT1", "Araw"], ["PT1"])
            V(lambda e, t=t: e.tensor_tensor(out=wgt, in0=wgt, in1=g_all[:, t, :], op=ALU.mult), ["PT1", f"M:g_{t}"], ["PT1"])
            for k in range(128):
                buf, bk = ring[rc[0] % NR]
                rc[0] += 1
                P.op(POOL, lambda e, buf=buf, t=t, k=k: e.indirect_dma_start(out=buf, out_offset=None, in_=V_d,
                                                                              in_offset=bass.IndirectOffsetOnAxis(ap=idx_all[:, t, k:k + 1], axis=0)),
                     [f"S:idx_{t}"], [bk], dma=True)
                acc = accs[k % 2]
                ak = acck[k % 2]
                if k < 2:
                    V(lambda e, buf=buf, k=k, acc=acc: e.tensor_scalar(out=acc, in0=buf, scalar1=wgt[:, k:k + 1], scalar2=None, op0=ALU.mult), [bk, "PT1"], [ak])
                else:
                    V(lambda e, buf=buf, k=k, acc=acc: e.scalar_tensor_tensor(out=acc, in0=buf, scalar=wgt[:, k:k + 1], in1=acc, op0=ALU.mult, op1=ALU.add),
                      [bk, "PT1", ak], [ak])
            V(lambda e: e.tensor_tensor(out=accs[0], in0=accs[0], in1=accs[1], op=ALU.add), acck, [acck[0]])
            V(lambda e: e.tensor_tensor(out=accs[0], in0=accs[0], in1=g2, op=ALU.mult), [acck[0]] + g2k, [acck[0]])
            V(lambda e, t=t: e.tensor_tensor(out=xres[:, t, :], in0=xres[:, t, :], in1=accs[0], op=ALU.add), [acck[0], f"x{t}"], [f"x{t}"])
        P.retire(*SHARED)

    def final_norm():
        P.dma(SP, tmpf[2], D["norm_final"].partition_broadcast(128), writes=["tmpf2"])
        for t in range(8):
            junk = tmpf[3][:, 0:512].bitcast(BF16)
            A(lambda e, t=t: e.activation(out=junk, in_=xres[:, t, :], func=AF.Square, accum_out=ssq[:, t:t + 1]), [f"x{t}"], ["tmpf3a", "tmpf3b", f"ssq{t}"])
            A(lambda e, t=t: e.activation(out=std[:, t:t + 1], in_=ssq[:, t:t + 1], func=AF.Sqrt, scale=1.0 / 1024, bias=EPS), [f"ssq{t}"], [f"std{t}"])
            V(lambda e, t=t: e.reciprocal(out=rstd[:, t:t + 1], in_=std[:, t:t + 1]), [f"std{t}"], [f"rstd{t}"])
            tf = tmpf[t % 2]
            V(lambda e, t=t, tf=tf: e.scalar_tensor_tensor(out=tf, in0=xres[:, t, :], scalar=rstd[:, t:t + 1], in1=tmpf[2], op0=ALU.mult, op1=ALU.mult),
              [f"x{t}", f"rstd{t}", "tmpf2"], [f"tmpf{t % 2}", f"tmpf{t % 2}v"])
            P.dma(SP, D["y"][t * 128:(t + 1) * 128, :], tf, reads=[f"tmpf{t % 2}", f"tmpf{t % 2}v"], is_output=True)

    def zero_outputs(names):
        z = tmpf[3]
        V(lambda e: e.memset(z, 0.0), [], ["tmpf3a", "tmpf3b"])
        for n in names:
            flat = D[n]
            tot = 1
            for s in flat.shape:
                tot *= s
            nd = len(flat.shape)
            if nd == 3:
                v = flat.rearrange("a b c -> (a b c)")
            elif nd == 4:
                v = flat.rearrange("a b c d -> (a b c d)")
            else:
                v = flat.rearrange("a b -> (a b)")
            v = v.rearrange("(p n) -> p n", p=128)
            per = tot // 128
            for c0 in range(0, per, 1024):
                w_ = min(1024, per - c0)
                P.dma(SP, v[:, c0:c0 + w_], z[:, 0:w_], reads=["tmpf3a", "tmpf3b"], is_output=True)

    if STOP_AFTER == "l1only":
        zero_outputs(["nak", "nav", "nsf", "nsb"])
        layer1()
        dbg_dump(2)
        final_norm()
    else:
        layer0()
        dbg_dump(0)
    if STOP_AFTER == "l1only":
        pass
    elif STOP_AFTER == "mix0":
        zero_outputs(["nck", "ncv", "ndk", "ndv"])
        final_norm()
    elif STOP_AFTER == "peer0":
        peer(0)
        dbg_dump(1)
        zero_outputs(["nck", "ncv", "ndk", "ndv"])
        final_norm()
    elif STOP_AFTER == "mix1":
        layer1()
        dbg_dump(2)
        final_norm()
    else:
        peer(0)
        dbg_dump(1)
        layer1()
        dbg_dump(2)
        peer(1)
        dbg_dump(3)
        final_norm()
    counts = P.emit()
    if DEBUG:
        print("ops per engine:", counts)
    es.close()
    return nc


_CACHE = {}


def _host_constants(is_sample):
    bf = ml_dtypes.bfloat16
    c = {}
    if is_sample:
        rows = 1024 // 64
        row = np.repeat(np.arange(rows, dtype=np.float32), 64)
        col = np.tile(np.arange(64, dtype=np.float32), rows)
        inv = (np.float32(10000.0) ** (-np.arange(16, dtype=np.float32) / np.float32(16))).astype(np.float32)
        ang = np.concatenate([row[:, None] * inv, col[:, None] * inv], axis=-1).astype(np.float32)
        c["rope_cos"] = np.cos(ang).astype(np.float32)
        c["rope_sin"] = np.sin(ang).astype(np.float32)
        mvA = np.zeros((10, 1024), np.float32)
        mvD = np.zeros((10, 1024), np.float32)
        for kb in range(2, 10):
            for qb in range(8):
                if abs((kb - 2) - qb) > 1:
                    mvD[kb, qb * 128:(qb + 1) * 128] = NEG
        kj = np.arange(128)[:, None]
        qi = np.arange(128)[None, :]
        triA = np.where(kj >= qi, 0.0, NEG)
        triB = np.where(kj <= qi, 0.0, NEG)
        flags = np.ones((1, 16), np.float32)
    else:
        c["rope_cos"] = np.ones((1024, 32), np.float32)
        c["rope_sin"] = np.zeros((1024, 32), np.float32)
        mvA = np.zeros((10, 1024), np.float32)
        mvA[0:2, :] = NEG
        for kb in range(2, 10):
            for qb in range(8):
                if (kb - 2) // 2 != qb // 2:
                    mvA[kb, qb * 128:(qb + 1) * 128] = NEG
        mvD = mvA.copy()
        triA = np.zeros((128, 128), np.float32)
        triB = np.zeros((128, 128), np.float32)
        flags = np.zeros((1, 16), np.float32)
        for cidx in range(8):
            flags[0, cidx] = 1.0 if cidx % 2 == 1 else 0.0
            flags[0, 8 + cidx] = 1.0 if cidx % 2 == 0 else 0.0
    c["mvA"] = mvA.astype(bf)
    c["mvD"] = mvD.astype(bf)
    c["triA"] = triA.astype(bf)
    c["triB"] = triB.astype(bf)
    c["flags"] = flags
    sel = np.zeros((10, 10, 128), np.float32)
    for kb in range(10):
        sel[kb, kb, :] = 1.0
    c["sel"] = sel.astype(bf)
    selK = np.zeros((10, 1280), np.float32)
    for kb in range(10):
        selK[kb, kb * 128:(kb + 1) * 128] = 1.0
    c["selK"] = selK.astype(bf)
    c["identb"] = np.eye(128, dtype=np.float32).astype(bf)
    c["identf"] = np.eye(128, dtype=np.float32)
    r = np.arange(128, dtype=np.float32)[:, None]
    p = np.arange(128, dtype=np.float32)[None, :]
    c["diff"] = (p - r).astype(np.float32)
    c["keepf"] = (p >= r).astype(np.float32)
    c["keepb"] = (r > p).astype(np.float32)
    pp = np.arange(128, dtype=np.float32)
    c["cols"] = np.stack([pp + 1, 128 - pp, 127 - pp, pp], axis=1).astype(np.float32)
    c["iota128"] = np.broadcast_to(np.arange(128, dtype=np.int32)[None, :], (128, 128)).copy()
    return c


def kernel(x_prompt, x_sample, c, c_ctx, cache_a_k, cache_a_v, state_ret_fwd, state_ret_bwd,
           cache_c_k, cache_c_v, cache_d_k, cache_d_v, mod_w, mod_b, norm_mix, norm_ffn, norm_final,
           ev_w_in, ev_w_out, a_q_norm, a_k_norm, ret_decay_fwd, ret_decay_bwd,
           od_w_in, od_w_out, c_lambda_q1, c_lambda_k1, c_lambda_q2, c_lambda_k2, c_subln, d_sink,
           peer_w_q, peer_subkeys, peer_u, peer_v):
    f = lambda a: np.ascontiguousarray(np.asarray(a), dtype=np.float32)
    if "nc" not in _CACHE:
        _CACHE["nc"] = build_program()
    nc = _CACHE["nc"]
    shared = {
        "mod_w": f(mod_w), "mod_b": f(mod_b), "norm_mix": f(norm_mix), "norm_ffn": f(norm_ffn),
        "norm_final": f(norm_final).reshape(1, 1024),
        "ev_w_in": f(ev_w_in)[0], "ev_w_out": f(ev_w_out)[0],
        "a_q_norm": f(a_q_norm).reshape(1, 64), "a_k_norm": f(a_k_norm).reshape(1, 64),
        "decay": np.concatenate([f(ret_decay_fwd).reshape(1, 8), f(ret_decay_bwd).reshape(1, 8)], axis=1),
        "od_w_in": f(od_w_in)[0], "od_w_out": f(od_w_out)[0],
        "lam4": np.concatenate([f(c_lambda_q1), f(c_lambda_k1), f(c_lambda_q2), f(c_lambda_k2)], axis=0),
        "subln": f(c_subln).reshape(1, 128), "sink": f(d_sink).reshape(1, 8),
        "w_q": f(peer_w_q), "subkeys": f(peer_subkeys).reshape(2, 16, 128, 128),
        "u0": f(peer_u)[0], "u1": f(peer_u)[1], "v0": f(peer_v)[0], "v1": f(peer_v)[1],
    }
    xp = f(x_prompt)
    xs = f(x_sample)
    in_maps = []
    for ci in range(8):
        m = dict(shared)
        samp = ci < 4
        m.update(_host_constants(samp))
        if samp:
            b = ci
            m["x"] = xs[b]
            m["cvec"] = f(c)[b:b + 1]
            m["cak"] = f(cache_a_k)[b, 0]; m["cav"] = f(cache_a_v)[b, 0]
            m["s0f"] = f(state_ret_fwd)[b, 0]; m["s0b"] = f(state_ret_bwd)[b, 0]
            m["cck"] = f(cache_c_k)[b, 0]; m["ccv"] = f(cache_c_v)[b, 0]
            m["cdk"] = f(cache_d_k)[b, 0]; m["cdv"] = f(cache_d_v)[b, 0]
        else:
            g = ci - 4
            m["x"] = xp[4 * g:4 * g + 4].reshape(1024, 1024)
            m["cvec"] = f(c_ctx).reshape(1, 1024)
            m["cak"] = np.zeros((2, 256, 64), np.float32); m["cav"] = np.zeros((2, 256, 64), np.float32)
            m["s0f"] = np.zeros((8, 64, 64), np.float32); m["s0b"] = np.zeros((8, 64, 64), np.float32)
            m["cck"] = np.zeros((8, 256, 64), np.float32); m["ccv"] = np.zeros((4, 256, 128), np.float32)
            m["cdk"] = np.zeros((2, 256, 64), np.float32); m["cdv"] = np.zeros((2, 256, 64), np.float32)
        if SMALL_UV:
            for k in ("u0", "u1", "v0", "v1"):
                m[k] = m[k][:128]
        in_maps.append({k: np.ascontiguousarray(v) for k, v in m.items()})
    res = run_bass_kernel_spmd(nc, in_maps, core_ids=list(range(8)))
    R = res.results
    _CACHE["last"] = R
    y_sample = np.stack([R[ci]["y"] for ci in range(4)], axis=0)
    y_prompt = np.concatenate([R[ci]["y"].reshape(4, 256, 1024) for ci in range(4, 8)], axis=0)

    def kvout(name, nh, dv):
        parts = [R[ci][name].reshape(nh, 4, 256, dv).transpose(1, 0, 2, 3)[:, None] for ci in range(4, 8)]
        return np.ascontiguousarray(np.concatenate(parts, axis=0), dtype=np.float32)

    def stout(name):
        parts = [R[ci][name].reshape(4, 1, 8, 64, 64) for ci in range(4, 8)]
        return np.ascontiguousarray(np.concatenate(parts, axis=0), dtype=np.float32)

    return (y_prompt.astype(np.float32), y_sample.astype(np.float32),
            kvout("nak", 2, 64), kvout("nav", 2, 64), stout("nsf"), stout("nsb"),
            kvout("nck", 8, 64), kvout("ncv", 4, 128), kvout("ndk", 2, 64), kvout("ndv", 2, 64))
```
